# Optimizing a Trainium2 kernel written in Bass

```python
import jax, jax.numpy as jnp
from jax import lax
import numpy as np

D_MODEL = 1024
BATCH = 8
SEQ = 2048
DEPTH = 1

PLE_DIM = 256
EPS = 1e-6
GLA_HEADS = 4
GLA_DK = 64
GLA_DV = 128
GLA_LOWRANK = 16
GLA_TAU = 16.0
GLA_CHUNK = 64
FOX_HEADS = 8
FOX_DH = 64
FOX_BLOCK = 128
D_FF = 2816
CONV_W = 3

GLA_QK_W = GLA_HEADS * GLA_DK
GLA_V_W = GLA_HEADS * GLA_DV
FOX_W = FOX_HEADS * FOX_DH
MIX_W = GLA_V_W + FOX_W
IN_SPLIT_SIZES = (GLA_QK_W, GLA_QK_W, GLA_V_W, GLA_V_W, GLA_LOWRANK, FOX_W, FOX_W, FOX_W, FOX_HEADS)
IN_W = 2 * GLA_QK_W + 2 * GLA_V_W + GLA_LOWRANK + 3 * FOX_W + FOX_HEADS

kernel_name = "hymba_gla_fox_convffn_ple"


def rms_norm(x, gain):
    xf = x.astype(jnp.float32)
    y = xf * lax.rsqrt(jnp.mean(xf * xf, axis=-1, keepdims=True) + EPS)
    return (y * gain.astype(jnp.float32)).astype(x.dtype)


def split_cols(t, sizes):
    offs = np.cumsum(sizes)[:-1].tolist()
    return jnp.split(t, offs, axis=-1)


def gla_mixer(q, k, v, lr, og, lr_w, lr_b, onorm_g):
    B, S, _ = q.shape
    N = S // GLA_CHUNK
    f32 = jnp.float32
    log_a = jax.nn.log_sigmoid((lr @ lr_w + lr_b).astype(f32)) / GLA_TAU

    def chunks(t, d):
        return t.astype(f32).reshape(B, N, GLA_CHUNK, GLA_HEADS, d).transpose(0, 3, 1, 2, 4)

    qc = chunks(q, GLA_DK) * (GLA_DK ** -0.5)
    kc = chunks(k, GLA_DK)
    vc = chunks(v, GLA_DV)
    b = jnp.cumsum(chunks(log_a, GLA_DK), axis=3)
    b_last = b[:, :, :, -1:, :]
    qe = qc * jnp.exp(b)
    ke = kc * jnp.exp(-b)
    kd = kc * jnp.exp(b_last - b)
    causal = jnp.tril(jnp.ones((GLA_CHUNK, GLA_CHUNK), dtype=bool))
    A = jnp.where(causal, jnp.einsum('bhnck,bhnsk->bhncs', qe, ke), 0.0)
    o_intra = jnp.einsum('bhncs,bhnsv->bhncv', A, vc)
    dS = jnp.einsum('bhnck,bhncv->nbhkv', kd, vc)
    decay = jnp.exp(b_last[:, :, :, 0, :]).transpose(2, 0, 1, 3)

    def step(state, inp):
        d, ds = inp
        return state * d[..., None] + ds, state

    s0 = jnp.zeros((B, GLA_HEADS, GLA_DK, GLA_DV), f32)
    _, s_prev = lax.scan(step, s0, (decay, dS))
    o_inter = jnp.einsum('bhnck,nbhkv->bhncv', qe, s_prev)
    o = (o_intra + o_inter).transpose(0, 2, 3, 1, 4).reshape(B, S, GLA_HEADS, GLA_DV)
    o = rms_norm(o, onorm_g) * jax.nn.silu(og.astype(f32).reshape(B, S, GLA_HEADS, GLA_DV))
    return o.reshape(B, S, GLA_V_W).astype(q.dtype)


def fox_mixer(q, k, v, f_logit, b_f, qn_g, kn_g):
    B, S, _ = q.shape
    f32 = jnp.float32
    qh = rms_norm(q.reshape(B, S, FOX_HEADS, FOX_DH), qn_g).transpose(0, 2, 1, 3)
    kh = rms_norm(k.reshape(B, S, FOX_HEADS, FOX_DH), kn_g).transpose(0, 2, 1, 3)
    vh = v.reshape(B, S, FOX_HEADS, FOX_DH).transpose(0, 2, 1, 3)
    c = jnp.cumsum(jax.nn.log_sigmoid((f_logit + b_f).astype(f32)), axis=1).transpose(0, 2, 1)
    scale = FOX_DH ** -0.5
    outs = []
    for i in range(S // FOX_BLOCK):
        q0, end = i * FOX_BLOCK, (i + 1) * FOX_BLOCK
        qs = qh[:, :, q0:end]
        ks = kh[:, :, :end]
        vs = vh[:, :, :end]
        logits = jnp.einsum('bhqd,bhkd->bhqk', qs, ks).astype(f32) * scale
        logits = logits + c[:, :, q0:end, None] - c[:, :, None, :end]
        qpos = q0 + jnp.arange(FOX_BLOCK)
        kpos = jnp.arange(end)
        logits = jnp.where(kpos[None, :] <= qpos[:, None], logits, -jnp.inf)
        probs = jax.nn.softmax(logits, axis=-1)
        outs.append(jnp.einsum('bhqk,bhkd->bhqd', probs.astype(vs.dtype), vs))
    o = jnp.concatenate(outs, axis=2)
    return o.transpose(0, 2, 1, 3).reshape(B, S, FOX_W)


def conv_ffn(h, w_up, conv_w, conv_b, w_down):
    S = h.shape[1]
    u = h @ w_up
    u_pad = jnp.pad(u, ((0, 0), (CONV_W - 1, 0), (0, 0)))
    u_c = conv_b + sum(u_pad[:, j:j + S] * conv_w[j] for j in range(CONV_W))
    gate, val = jnp.split(u_c, 2, axis=-1)
    return (jax.nn.silu(gate) * val) @ w_down


def setup_inputs(seed: int = 0) -> dict:
    key = jax.random.key(seed)
    ks = jax.random.split(key, 24)
    f32 = jnp.float32

    def nrm(k, shape, scale):
        return jax.random.normal(k, shape, f32) * scale

    L = DEPTH
    return {
        "x": nrm(ks[0], (BATCH, SEQ, D_MODEL), 1.0),
        "p": nrm(ks[1], (DEPTH, BATCH, SEQ, PLE_DIM), 1.0),
        "norm1_g": 1.0 + nrm(ks[2], (L, D_MODEL), 0.02),
        "w_in": nrm(ks[3], (L, D_MODEL, IN_W), D_MODEL ** -0.5),
        "gla_lr_w": nrm(ks[4], (L, GLA_LOWRANK, GLA_QK_W), GLA_LOWRANK ** -0.5),
        "gla_lr_b": nrm(ks[5], (L, GLA_QK_W), 0.1),
        "gla_onorm_g": 1.0 + nrm(ks[6], (L, GLA_DV), 0.02),
        "fox_b_f": nrm(ks[7], (L, FOX_HEADS), 0.1),
        "fox_qnorm_g": 1.0 + nrm(ks[8], (L, FOX_DH), 0.02),
        "fox_knorm_g": 1.0 + nrm(ks[9], (L, FOX_DH), 0.02),
        "w_o": nrm(ks[10], (L, MIX_W, D_MODEL), MIX_W ** -0.5),
        "norm2_g": 1.0 + nrm(ks[11], (L, D_MODEL), 0.02),
        "w_up": nrm(ks[12], (L, D_MODEL, 2 * D_FF), D_MODEL ** -0.5),
        "conv_w": nrm(ks[13], (L, CONV_W, 2 * D_FF), CONV_W ** -0.5),
        "conv_b": nrm(ks[14], (L, 2 * D_FF), 0.02),
        "w_down": nrm(ks[15], (L, D_FF, D_MODEL), D_FF ** -0.5),
        "norm3_g": 1.0 + nrm(ks[16], (L, D_MODEL), 0.02),
        "w_pe": nrm(ks[17], (L, PLE_DIM, D_MODEL), PLE_DIM ** -0.5),
        "pe_norm_g": 1.0 + nrm(ks[18], (L, D_MODEL), 0.02),
        "w_pg": nrm(ks[19], (L, D_MODEL, D_MODEL), D_MODEL ** -0.5),
        "b_pg": nrm(ks[20], (L, D_MODEL), 0.02),
    }


def reference(x, p, norm1_g, w_in, gla_lr_w, gla_lr_b, gla_onorm_g, fox_b_f, fox_qnorm_g,
              fox_knorm_g, w_o, norm2_g, w_up, conv_w, conv_b, w_down, norm3_g, w_pe,
              pe_norm_g, w_pg, b_pg):
    for i in range(DEPTH):
        h = rms_norm(x, norm1_g[i])
        proj = h @ w_in[i]
        g_q, g_k, g_v, g_og, g_lr, f_q, f_k, f_v, f_f = split_cols(proj, IN_SPLIT_SIZES)
        y_gla = gla_mixer(g_q, g_k, g_v, g_lr, g_og, gla_lr_w[i], gla_lr_b[i], gla_onorm_g[i])
        y_fox = fox_mixer(f_q, f_k, f_v, f_f, fox_b_f[i], fox_qnorm_g[i], fox_knorm_g[i])
        x = x + jnp.concatenate([y_gla, y_fox], axis=-1) @ w_o[i]
        x = x + conv_ffn(rms_norm(x, norm2_g[i]), w_up[i], conv_w[i], conv_b[i], w_down[i])
        e = rms_norm(p[i] @ w_pe[i], pe_norm_g[i])
        gate = jax.nn.sigmoid(rms_norm(x, norm3_g[i]) @ w_pg[i] + b_pg[i])
        x = x + gate * e
    return x
```

```python
import math
from contextlib import ExitStack

import numpy as np
import concourse.bass as bass
import concourse.mybir as mybir
from concourse.bass_utils import run_bass_kernel_spmd

F32 = mybir.dt.float32
BF16 = mybir.dt.bfloat16
AF = mybir.ActivationFunctionType
ALU = mybir.AluOpType
AX = mybir.AxisListType

S_LEN = 2048
D = 1024
NT = 16
NB = 4
KC = 8
EPS = 1e-6
DFF = 2816
NJ = 22
NCORES = 8

ENGINES = ("pe", "act", "dve", "pool", "sp")
NDMA_SEM = 12
SCHED_WINDOW = 200
SEM_LAT = 600.0
REORDER = True


class _Op:
    __slots__ = ("eng", "emit", "alldeps", "deps", "sig", "cnt", "idx", "dma", "dsem", "dgen",
                 "prev_dma", "dur", "seg", "pidx")

    def __init__(self, eng, emit, dma, dur):
        self.eng = eng
        self.emit = emit
        self.alldeps = ()
        self.deps = []
        self.sig = False
        self.cnt = 0
        self.idx = 0
        self.dma = dma
        self.dsem = None
        self.dgen = 0
        self.prev_dma = None
        self.dur = dur
        self.seg = 0
        self.pidx = 0


class Sched:
    def __init__(self, same_engine_sync=True):
        self.prog = []
        self.ops = {e: [] for e in ENGINES}
        self.last_w = {}
        self.readers = {}
        self.seg = 0
        self.same_engine_sync = same_engine_sync

    def add(self, eng, emit, reads=(), writes=(), dma=False, dur=100.0):
        op = _Op(eng, emit, dma, dur)
        op.seg = self.seg
        op.pidx = len(self.prog)
        deps = set()
        for r in reads:
            w = self.last_w.get(r)
            if w is not None:
                deps.add(w)
        for r in writes:
            w = self.last_w.get(r)
            if w is not None:
                deps.add(w)
            for rd in self.readers.get(r, ()):
                deps.add(rd)
        deps.discard(op)
        for r in reads:
            self.readers.setdefault(r, []).append(op)
        for r in writes:
            self.last_w[r] = op
            self.readers[r] = []
        op.alldeps = tuple(deps)
        self.prog.append(op)
        return op

    def barrier(self):
        self.seg += 1
        self.last_w = {}
        self.readers = {}

    def _schedule(self, ops):
        pend = {e: [] for e in ENGINES}
        for op in ops:
            pend[op.eng].append(op)
        order = {e: [] for e in ENGINES}
        if not REORDER:
            return pend
        inseg = set(ops)
        succ = {}
        for op in ops:
            for d in op.alldeps:
                if d in inseg:
                    succ.setdefault(d, []).append(op)
        blev = {}
        for op in reversed(ops):
            m = 0.0
            for s_ in succ.get(op, ()):
                lat = SEM_LAT if (s_.eng != op.eng or op.dma) else 0.0
                v = blev[s_] + lat
                if v > m:
                    m = v
            blev[op] = op.dur + m
        fin = {}
        free = {e: 0.0 for e in ENGINES}
        left = len(ops)
        W = SCHED_WINDOW
        while left:
            best = None
            for e in ENGINES:
                lst = pend[e]
                if not lst:
                    continue
                ef = free[e]
                cand = None
                for k in range(min(W, len(lst))):
                    op = lst[k]
                    rt = 0.0
                    ok = True
                    for d in op.alldeps:
                        if d not in inseg:
                            continue
                        ft = fin.get(d)
                        if ft is None:
                            ok = False
                            break
                        if d.eng != e or d.dma:
                            ft += SEM_LAT
                        if ft > rt:
                            rt = ft
                    if not ok:
                        continue
                    st = rt if rt > ef else ef
                    key = (st, -blev[op], k)
                    if cand is None or key < cand[0]:
                        cand = (key, op)
                if cand is not None and (best is None or cand[0] < best[0]):
                    best = (cand[0], e, cand[1])
            (st, _nb, k), e, op = best
            pend[e].pop(k)
            if op.dma:
                issue = 1200.0 if e == "pool" else 100.0
                fin[op] = st + issue + op.dur
                free[e] = st + issue
            else:
                fin[op] = st + op.dur
                free[e] = st + op.dur
            order[e].append(op)
            left -= 1
        self.est_time = getattr(self, "est_time", 0.0) + max(list(fin.values()) + [0.0])
        return order

    def _prune(self, eng, deps):
        best = {}
        dbest = {}
        for d in deps:
            if d.dma:
                k = (d.eng, d.dsem)
                b = dbest.get(k)
                if b is None or d.dgen > b.dgen:
                    dbest[k] = d
            else:
                if d.eng == eng and (eng == "pe" or not self.same_engine_sync):
                    continue
                b = best.get(d.eng)
                if b is None or d.idx > b.idx:
                    best[d.eng] = d
        return list(best.values()) + list(dbest.values())

    def finalize(self):
        nseg = self.seg + 1
        segs = [[] for _ in range(nseg)]
        for op in self.prog:
            segs[op.seg].append(op)
        ndma = {e: 0 for e in ENGINES}
        hist = {e: [] for e in ENGINES}
        for si, sops in enumerate(segs):
            order = self._schedule(sops)
            seg_dmas = []
            lasts = []
            for e in ENGINES:
                for op in order[e]:
                    op.idx = len(self.ops[e])
                    self.ops[e].append(op)
                    if op.dma:
                        k = ndma[e]
                        op.dsem = k % NDMA_SEM
                        op.dgen = k // NDMA_SEM + 1
                        op.prev_dma = hist[e][k - NDMA_SEM] if k >= NDMA_SEM else None
                        hist[e].append(op)
                        ndma[e] += 1
                        seg_dmas.append(op)
                if order[e]:
                    lasts.append(order[e][-1])
            for op in sops:
                op.deps = self._prune(op.eng, [d for d in op.alldeps if d.seg == op.seg])
            for e in ENGINES:
                bop = _Op(e, None, False, 0.0)
                bop.seg = si
                bop.idx = len(self.ops[e])
                bop.deps = self._prune("__none__", [d for d in lasts + seg_dmas if not (d.eng == e and not d.dma)])
                self.ops[e].append(bop)
        for e in ENGINES:
            for op in self.ops[e]:
                for d in op.deps:
                    d.sig = True
        for e in ENGINES:
            c = 0
            for op in self.ops[e]:
                if op.sig and not op.dma and op.emit is not None:
                    c += 1
                op.cnt = c

    def emit_engine(self, eng_name, eng, sems, dsems):
        waited = {}
        for op in self.ops[eng_name]:
            need = []
            for d in op.deps:
                if d.dma:
                    need.append((("d", d.eng, d.dsem), 16 * d.dgen, dsems[d.eng][d.dsem]))
                else:
                    need.append((("c", d.eng), d.cnt, sems[d.eng]))
            if op.dma and op.prev_dma is not None:
                d = op.prev_dma
                need.append((("d", d.eng, d.dsem), 16 * d.dgen, dsems[d.eng][d.dsem]))
            for key, val, sem in need:
                if waited.get(key, 0) >= val:
                    continue
                eng.wait_ge(sem, val)
                waited[key] = val
            if op.emit is None:
                continue
            ins = op.emit(eng)
            if op.dma:
                ins.then_inc(dsems[eng_name][op.dsem], 16)
            elif op.sig:
                ins.then_inc(sems[eng_name], 1)


class _Rot:
    def __init__(self, items):
        self.items = list(items)
        self.i = 0

    def next(self):
        it = self.items[self.i % len(self.items)]
        self.i += 1
        return it


def _fsize(ap):
    n = 1
    for s_ in ap.shape[1:]:
        n *= s_
    return n


FFN_GROUPS = [(0, 8), (8, 15), (15, 22)]
GMAX = 8

DRAM_INPUTS = [
    ("x", [S_LEN, D]), ("p", [S_LEN, 256]),
    ("g1", [128, 8]), ("g2", [128, 8]), ("g3", [128, 8]),
    ("w_gla", [128, 8, 1552]), ("w_fqk", [4, 128, 8, 256]), ("w_fv", [128, 8, 512]), ("w_f", [128, 8, 8]),
    ("lr_w", [16, 256]), ("lr_b", [128, 2]), ("gon_bc", [128, 512]),
    ("b_f", [8, 1]), ("gq", [64, 1]), ("gk", [64, 1]),
    ("wo_g", [128, 4, 1024]), ("wo_f", [128, 4, 1024]),
    ("w_up", [NJ, 128, 8, 256]), ("cwg", [128, NJ, 4]), ("cwv", [128, NJ, 4]),
    ("w_down", [NJ, 128, 1024]),
    ("w_pe", [128, 2, 1024]), ("w_pg", [128, 8, 1024]),
    ("bpg_bc", [128, 1024]), ("peg_bc", [128, 1024]),
    ("ident", [128, 128]), ("tri", [128, 128]), ("negi", [128, 128]), ("gmask", [128, 256]),
]

PHASES = ["hT", "gla", "fox", "wo", "ffn", "full"]
GLA_LIMIT = 6


def build_program(stop="full"):
    lvl = PHASES.index(stop)
    nc = bass.Bass("TRN2", target_bir_lowering=False)
    dr = {}
    for name, shape in DRAM_INPUTS:
        dr[name] = nc.dram_tensor(name, list(shape), F32, kind="ExternalInput").ap()
    out_d = nc.dram_tensor("out", [S_LEN, D], F32, kind="ExternalOutput").ap()
    dbg_d = {}

    S = Sched()
    es_all = ExitStack()

    def mk_alloc(es):
        def sb(name, shape, dt):
            return es.enter_context(nc.sbuf_tensor(name, list(shape), dt))
        return sb

    sbp = mk_alloc(es_all)
    banks = [es_all.enter_context(nc.psum_tensor(f"bank{i}", [128, 512], F32)) for i in range(8)]
    sems = {e: es_all.enter_context(nc.semaphore("s_" + e)) for e in ENGINES}
    dsems = {e: [es_all.enter_context(nc.semaphore(f"d_{e}{i}")) for i in range(NDMA_SEM)]
             for e in ("sp", "pool", "act")}

    def _bytes(ap):
        return ap.shape[0] * _fsize(ap) * (2 if ap.dtype == BF16 else 4)

    def dma(q, out, in_, reads=(), writes=()):
        return S.add(q, lambda e: e.dma_start(out=out, in_=in_), reads, writes, dma=True,
                     dur=5000.0 + _bytes(out) / 100.0)

    def mm(out, lhsT, rhs, start, stop, reads=(), writes=()):
        n = max(_fsize(rhs), 64)
        d = n / 2.4 * (4.0 if rhs.dtype == F32 else 1.0) + 12.0
        if lhsT.shape[0] < 128:
            d = 2.0 * d + 50.0
        return S.add("pe", lambda e: e.matmul(out, lhsT=lhsT, rhs=rhs, start=start, stop=stop), reads, writes, dur=d)

    def tr(out, in_, reads=(), writes=()):
        return S.add("pe", lambda e: e.transpose(out=out, in_=in_, identity=identb[:]),
                     list(reads) + ["ident"], writes, dur=70.0)

    def act(out, in_, func, reads=(), writes=(), scale=1.0, bias=None, accum_out=None):
        kw = {}
        if bias is not None:
            kw["bias"] = bias
        if accum_out is not None:
            kw["accum_out"] = accum_out
        d = _fsize(in_) / 1.2 + 200.0
        return S.add("act", lambda e: e.activation(out=out, in_=in_, func=func, scale=scale, **kw), reads, writes, dur=d)

    def _vdur(eng, n, mult=1.0):
        if eng == "pool":
            return n * mult / 0.55 + 220.0
        return n * mult / 0.96 + 80.0

    def _is_psum(ap):
        return "psum" in str(ap.space).lower() or "bank" in str(ap.name)

    def tt(eng, out, in0, in1, op, reads=(), writes=()):
        m = 1.0 if (eng == "pool" or _is_psum(in0) or _is_psum(in1)) else 1.8
        d = _vdur(eng, _fsize(out), m)
        if op == ALU.pow:
            d = 450.0 + 150.0 * _fsize(out)
        return S.add(eng, lambda e: e.tensor_tensor(out=out, in0=in0, in1=in1, op=op), reads, writes, dur=d)

    def ts(eng, out, in0, s1, s2, op0, op1=None, reads=(), writes=()):
        d = _vdur(eng, _fsize(out))
        if op1 is None:
            return S.add(eng, lambda e: e.tensor_scalar(out=out, in0=in0, scalar1=s1, scalar2=None, op0=op0), reads, writes, dur=d)
        return S.add(eng, lambda e: e.tensor_scalar(out=out, in0=in0, scalar1=s1, scalar2=s2, op0=op0, op1=op1), reads, writes, dur=d)

    def stt(out, in0, scalar, in1, op0, op1, reads=(), writes=()):
        m = 1.0 if (_is_psum(in0) or _is_psum(in1)) else 1.8
        return S.add("dve", lambda e: e.scalar_tensor_tensor(out=out, in0=in0, scalar=scalar, in1=in1, op0=op0, op1=op1), reads, writes,
                     dur=_vdur("dve", _fsize(out), m))

    def cp(eng, out, in_, reads=(), writes=()):
        if eng == "act":
            return act(out, in_, AF.Copy, reads, writes)
        return S.add(eng, lambda e: e.tensor_copy(out=out, in_=in_), reads, writes, dur=_vdur(eng, _fsize(out)))

    def memset(eng, ap, val, writes=()):
        return S.add(eng, lambda e: e.memset(ap, val), (), writes, dur=_vdur(eng, _fsize(ap), 0.5))

    def scan(out, d0, d1, reads=(), writes=()):
        return S.add("dve", lambda e: e.tensor_tensor_scan(out=out, data0=d0, data1=d1, initial=0.0,
                                                           op0=ALU.mult, op1=ALU.add), reads, writes,
                     dur=_vdur("dve", _fsize(out), 2.0))

    def reduce_x(out, in_, reads=(), writes=()):
        return S.add("dve", lambda e: e.tensor_reduce(out=out, in_=in_, axis=AX.X, op=ALU.add), reads, writes,
                     dur=_vdur("dve", _fsize(in_)))

    def recip(out, in_, reads=(), writes=()):
        return S.add("dve", lambda e: e.reciprocal(out=out, in_=in_), reads, writes, dur=_vdur("dve", _fsize(out)))

    def bk_bf(b):
        return banks[b][:].bitcast(BF16)

    def dump(name, ap_sb, shape, dt):
        d = nc.dram_tensor("dbg_" + name, list(shape), dt, kind="ExternalOutput").ap()
        dbg_d[name] = d
        dma("sp", d, ap_sb)

    identb = sbp("identb", [128, 128], BF16)
    trib = sbp("trib", [128, 128], BF16)
    negib = sbp("negib", [128, 128], BF16)
    gmaskf = sbp("gmaskf", [128, 256], F32)
    g1t = sbp("g1t", [128, 8], F32)
    g2t = sbp("g2t", [128, 8], F32)
    g3t = sbp("g3t", [128, 8], F32)
    neghalf = sbp("neghalf", [128, 16], F32)
    junk = sbp("junk", [128, 1024], BF16)
    st_ss = sbp("st_ss", [128, 16], F32)
    st_ms = sbp("st_ms", [128, 16], F32)
    st_rs = sbp("st_rs", [128, 16], F32)
    arena = sbp("arena", [128, 16384], F32)

    def aview(off, parts, shape_free, dt):
        n = 1
        for s_ in shape_free:
            n *= s_
        nbytes = n * (2 if dt == BF16 else 4)
        assert off % 4 == 0 and nbytes % 4 == 0 and off + nbytes <= 65536
        a = arena[0:parts, off // 4:(off + nbytes) // 4]
        if dt == BF16:
            a = a.bitcast(BF16)
        if len(shape_free) == 2:
            a = a.rearrange("p (a b) -> p a b", b=shape_free[1])
        elif len(shape_free) == 3:
            a = a.rearrange("p (a b c) -> p a b c", b=shape_free[1], c=shape_free[2])
        return a

    dma("pool", identb[:], dr["ident"], writes=["ident"])
    dma("pool", trib[:], dr["tri"], writes=["tri"])
    dma("pool", negib[:], dr["negi"], writes=["negi"])
    dma("sp", gmaskf[:], dr["gmask"], writes=["gmask"])
    dma("sp", g1t[:], dr["g1"], writes=["g1"])
    dma("sp", g2t[:], dr["g2"], writes=["g2"])
    dma("sp", g3t[:], dr["g3"], writes=["g3"])
    memset("pool", neghalf[:], -0.5, writes=["neghalf"])

    def norm_to_hT(src, gain, gkey, hT, hkey, tag, htm_bufs):
        htm = _Rot(htm_bufs)
        trb = _Rot(list(range(8)))
        for tb in range(NB):
            used = []
            for ti in range(4):
                t = tb * 4 + ti
                xa, xk = src(t)
                hb, hbk = htm.next()
                used.append((hb, hbk))
                act(junk[:], xa, AF.Square, reads=xk, writes=[(tag, "ss", t), "junk"], accum_out=st_ss[:, t:t + 1])
                ts("dve", st_ms[:, t:t + 1], st_ss[:, t:t + 1], 1.0 / D, EPS, ALU.mult, ALU.add,
                   reads=[(tag, "ss", t)], writes=[(tag, "ms", t)])
                tt("pool", st_rs[:, t:t + 1], st_ms[:, t:t + 1], neghalf[:, 0:1], ALU.pow,
                   reads=[(tag, "ms", t), "neghalf"], writes=[(tag, "rstd", t)])
                ts("dve", hb, xa, st_rs[:, t:t + 1], None, ALU.mult,
                   reads=list(xk) + [(tag, "rstd", t)], writes=[hbk])
            for c in range(KC):
                b = trb.next()
                for ti in range(4):
                    hb, hbk = used[ti]
                    tr(bk_bf(b)[:, ti * 128:(ti + 1) * 128], hb[:, c * 128:(c + 1) * 128],
                       reads=[hbk], writes=[("ps", b)])
                if c % 2 == 0:
                    act(hT[:, c, tb * 512:(tb + 1) * 512], bk_bf(b)[:, 0:512], AF.Copy,
                        reads=[("ps", b), gkey], writes=[(hkey, c, tb)], scale=gain[:, c:c + 1])
                else:
                    ts("dve", hT[:, c, tb * 512:(tb + 1) * 512], bk_bf(b)[:, 0:512], gain[:, c:c + 1], None, ALU.mult,
                       reads=[("ps", b), gkey], writes=[(hkey, c, tb)])

    es_h = ExitStack()
    hT = mk_alloc(es_h)("hT", [128, KC, S_LEN], BF16)
    es_mix = ExitStack()
    sbm = mk_alloc(es_mix)
    yTg = sbm("yTg", [128, 4, S_LEN], BF16)
    if lvl >= 2:
        wv = sbm("wv", [128, KC, 512], BF16)
        wqk = [sbm(f"wqk{i}", [128, KC, 256], BF16) for i in range(2)]
        wf = sbm("wf", [128, KC, 8], BF16)
        def fox_prefetch():
            dma("pool", wf[:], dr["w_f"], writes=["wf"])
            for c in range(KC):
                dma("pool", wv[:, c, :], dr["w_fv"][:, c, :], writes=[("wv", c)])
            for c in range(KC):
                dma("pool", wqk[0][:, c, :], dr["w_fqk"][0, :, c, :], writes=[("wqk", 0, c)])

    if lvl >= 1:
        es2 = ExitStack()
        sb2 = mk_alloc(es2)
        wg = sb2("wg", [128, KC, 1552], BF16)
        lrT = sb2("lrT", [16, S_LEN], BF16)
        lrw = sb2("lrw", [16, 256], BF16)
        lrb = sb2("lrb", [128, 2], F32)
        nlrb = sb2("nlrb", [128, 2], F32)
        gonbc = sb2("gonbc", [128, 512], F32)
        msk = sb2("msk", [128, S_LEN], BF16)
        dec = sb2("dec", [128, 2, 32], F32)
        qeT = sb2("qeT", [128, 2, S_LEN], BF16)
        keT = sb2("keT", [128, 2, S_LEN], BF16)
        etmp = [sb2(f"etmp{i}", [128, 512], F32) for i in range(6)]
        sst = sb2("sst", [128, 2, 256], F32)
        sbf = [sb2(f"sbf{i}", [128, 2, 256], BF16) for i in range(2)]
        atsb = [sb2(f"atsb{i}", [128, 256], BF16) for i in range(2)]
        sqt = [sb2(f"sqt{i}", [128, 512], F32) for i in range(2)]
        ssq = sb2("ssq", [128, NT, 4], F32)
        msq = sb2("msq", [128, NT, 4], F32)
        rsq = sb2("rsq", [128, NT, 4], F32)
        ytm = [sb2(f"ytm{i}", [128, 512], BF16) for i in range(2)]
        vg = aview(0, 128, [NT, 512], BF16)
        gsog = aview(16384, 128, [NT, 512], BF16)
        sp_ = aview(32768, 128, [2, S_LEN], F32)
        cs = aview(49152, 128, [2, S_LEN], F32)
        kdT = aview(32768, 128, [2, S_LEN], BF16)
        kdtm = aview(40960, 128, [NT, 256], BF16)
        erot = _Rot(list(range(6)))

        for c in range(KC):
            dma("pool", wg[:, c, :], dr["w_gla"][:, c, :], writes=[("wg", c)])
        dma("pool", lrw[:], dr["lr_w"], writes=["lrw"])
        dma("sp", lrb[:], dr["lr_b"], writes=["lrb"])
        dma("sp", gonbc[:], dr["gon_bc"], writes=["gonbc"])
        ts("dve", nlrb[:], lrb[:], -1.0, None, ALU.mult, reads=["lrb"], writes=["nlrb"])
        memset("pool", msk[:], 1.0, writes=["msk"])
        memset("pool", msk[:].rearrange("p (a b) -> p a b", b=64)[:, :, 0:1], 0.0, writes=["msk"])

    xts = [aview(i * 4096, 128, [D], F32) for i in range(3)]
    htm_bufs = [(aview(12288 + i * 2048, 128, [D], BF16), ("htm", i)) for i in range(8)]
    xrot = _Rot(list(range(3)))

    def src1(t):
        r = xrot.next()
        dma("sp", xts[r], dr["x"][t * 128:(t + 1) * 128, :], writes=[("xt", r)])
        return xts[r], [("xt", r)]

    norm_to_hT(src1, g1t, "g1", hT, "hT", "n1", htm_bufs)
    A1_KEYS = [("xt", r) for r in range(3)] + [("htm", i) for i in range(8)]
    if stop == "hT":
        S.barrier()
    if stop == "hT":
        dump("hT", hT[:], [128, KC, S_LEN], BF16)

    if lvl >= 1:
        prot = _Rot(list(range(8)))

        if GLA_LIMIT >= 1:
            for tb in range(NB):
                tbs = slice(tb * 512, (tb + 1) * 512)
                b = prot.next()
                for c in range(KC):
                    mm(banks[b][0:16, :], wg[:, c, 1536:1552], hT[:, c, tbs], c == 0, c == KC - 1,
                       reads=[("wg", c), ("hT", c, tb)], writes=[("ps", b)])
                cp("act", lrT[:, tbs], banks[b][0:16, :], reads=[("ps", b)], writes=[("lrT", tb)])
        if GLA_LIMIT >= 2:
            for f in range(2):
                for tb in range(NB):
                    tbs = slice(tb * 512, (tb + 1) * 512)
                    b = prot.next()
                    mm(banks[b][:, :], lrw[:, f * 128:(f + 1) * 128], lrT[:, tbs], True, True,
                       reads=["lrw", ("lrT", tb)], writes=[("ps", b)])
                    r = erot.next()
                    act(etmp[r][:], banks[b][:, :], AF.Exp, reads=[("ps", b), "nlrb"], writes=[("etmp", r)],
                        scale=-1.0, bias=nlrb[:, f:f + 1])
                    act(sp_[:, f, tbs], etmp[r][:], AF.Ln, reads=[("etmp", r)], writes=[("sp", f)], bias=1.0)
                scan(cs[:, f, :], msk[:], sp_[:, f, :], reads=[("sp", f), "msk"], writes=[("cs", f)])
                act(dec[:, f, :], cs[:, f, :].rearrange("p (n c) -> p n c", c=64)[:, :, 63],
                    AF.Exp, reads=[("cs", f)], writes=[("dec", f)], scale=-1.0 / 16)
            if lvl >= 2:
                fox_prefetch()
        if GLA_LIMIT >= 3:
            LNQ = math.log(0.125)
            for f in range(2):
                for tb in range(NB):
                    tbs = slice(tb * 512, (tb + 1) * 512)
                    r1, r2, r3 = erot.next(), erot.next(), erot.next()
                    csv = cs[:, f, tbs]
                    act(etmp[r1][:], csv, AF.Exp, reads=[("cs", f)], writes=[("etmp", r1)], scale=-1.0 / 16, bias=LNQ)
                    act(etmp[r2][:], csv, AF.Exp, reads=[("cs", f)], writes=[("etmp", r2)], scale=1.0 / 16)
                    cs3 = csv.rearrange("p (n c) -> p n c", c=64)
                    tt("dve", etmp[r3][:].rearrange("p (n c) -> p n c", c=64),
                       cs3[:, :, 63:64].to_broadcast([128, 8, 64]), cs3, ALU.subtract,
                       reads=[("cs", f)], writes=[("etmp", r3)])
                    act(etmp[r3][:], etmp[r3][:], AF.Exp, reads=[("etmp", r3)], writes=[("etmp", r3)], scale=-1.0 / 16)
                    b = prot.next()
                    for c in range(KC):
                        mm(banks[b][:, :], wg[:, c, f * 128:(f + 1) * 128], hT[:, c, tbs], c == 0, c == KC - 1,
                           reads=[("wg", c), ("hT", c, tb)], writes=[("ps", b)])
                    tt("dve", qeT[:, f, tbs], banks[b][:, :], etmp[r1][:], ALU.mult,
                       reads=[("ps", b), ("etmp", r1)], writes=[("qeT", f, tb)])
                    b = prot.next()
                    for c in range(KC):
                        mm(banks[b][:, :], wg[:, c, 256 + f * 128:256 + (f + 1) * 128], hT[:, c, tbs], c == 0, c == KC - 1,
                           reads=[("wg", c), ("hT", c, tb)], writes=[("ps", b)])
                    tt("dve", keT[:, f, tbs], banks[b][:, :], etmp[r2][:], ALU.mult,
                       reads=[("ps", b), ("etmp", r2)], writes=[("keT", f, tb)])
                    tt("dve", kdT[:, f, tbs], banks[b][:, :], etmp[r3][:], ALU.mult,
                       reads=[("ps", b), ("etmp", r3)], writes=[("kdT", f, tb), ("sp", 0), ("sp", 1)])
        if GLA_LIMIT >= 4:
            for t in range(NT):
                b = prot.next()
                for f in range(2):
                    tr(bk_bf(b)[:, f * 128:(f + 1) * 128], kdT[:, f, t * 128:(t + 1) * 128],
                       reads=[("kdT", f, t // 4)], writes=[("ps", b)])
                cp("act", kdtm[:, t, :], bk_bf(b)[:, 0:256], reads=[("ps", b)], writes=[("kdtm", t), ("sp", 0), ("sp", 1)])
        if GLA_LIMIT >= 5:
            for t in range(NT):
                tsl = slice(t * 128, (t + 1) * 128)
                b = prot.next()
                for c in range(KC):
                    mm(banks[b][:, :], hT[:, c, tsl], wg[:, c, 512:1024], c == 0, c == KC - 1,
                       reads=[("wg", c), ("hT", c, t // 4)], writes=[("ps", b)])
                cp("act", vg[:, t, :], banks[b][:, :], reads=[("ps", b)], writes=[("vg", t)] + A1_KEYS)
                b = prot.next()
                for c in range(KC):
                    mm(banks[b][:, :], hT[:, c, tsl], wg[:, c, 1024:1536], c == 0, c == KC - 1,
                       reads=[("wg", c), ("hT", c, t // 4)], writes=[("ps", b)])
                r = erot.next()
                act(etmp[r][:], banks[b][:, :], AF.Silu, reads=[("ps", b)], writes=[("etmp", r)])
                tt("pool", gsog[:, t, :], etmp[r][:], gonbc[:], ALU.mult,
                   reads=[("etmp", r), "gonbc"], writes=[("gsog", t)] + A1_KEYS)
        if GLA_LIMIT >= 6:
            A_BK, B_BK = 0, (1, 2)
            prot = _Rot([3, 4, 5, 6, 7])
            memset("pool", sbf[1][:], 0.0, writes=[("sbf", 1, 0), ("sbf", 1, 1)])
            for n in range(32):
                t, half = n // 2, n % 2
                r0 = half * 64
                rs = slice(r0, r0 + 64)
                ns = slice(n * 64, (n + 1) * 64)
                tb = n // 8
                dsb = []
                for hp in range(2):
                    b = prot.next()
                    dsb.append(b)
                    mm(banks[b][:, 0:256], kdtm[rs, t, hp * 128:(hp + 1) * 128], vg[rs, t, hp * 256:(hp + 1) * 256],
                       True, True, reads=[("kdtm", t), ("vg", t)], writes=[("ps", b)])
                ar = n % 2
                for i in range(2):
                    ab = prot.next()
                    ps_ = slice(i * 64, (i + 1) * 64)
                    for hp in range(2):
                        mm(banks[ab][rs, hp * 64:(hp + 1) * 64], keT[ps_, hp, ns], qeT[ps_, hp, ns], True, True,
                           reads=[("keT", hp, tb), ("qeT", hp, tb)], writes=[("ps", ab)])
                    tt("dve", atsb[ar][rs, i * 128:(i + 1) * 128], banks[ab][rs, 0:128], gmaskf[rs, 0:128], ALU.mult,
                       reads=[("ps", ab), "gmask"], writes=[("atsb", ar, i)])
                for h in range(4):
                    hp, i = h // 2, h % 2
                    ps_ = slice(i * 64, (i + 1) * 64)
                    mm(banks[A_BK][rs, h * 128:(h + 1) * 128], atsb[ar][rs, i * 128 + hp * 64:i * 128 + (hp + 1) * 64],
                       vg[rs, t, h * 128:(h + 1) * 128], True, True,
                       reads=[("atsb", ar, i), ("vg", t)], writes=[("ps", A_BK)])
                for h in range(4):
                    hp, i = h // 2, h % 2
                    ps_ = slice(i * 64, (i + 1) * 64)
                    mm(banks[B_BK[i]][rs, hp * 128:(hp + 1) * 128], qeT[ps_, hp, ns],
                       sbf[(n - 1) % 2][ps_, hp, i * 128:(i + 1) * 128], True, True,
                       reads=[("qeT", hp, tb), ("sbf", (n - 1) % 2, hp)], writes=[("ps", B_BK[i])])
                if n < 31:
                    for hp in range(2):
                        b = dsb[hp]
                        if n == 0:
                            cp("dve", sst[:, hp, :], banks[b][:, 0:256], reads=[("ps", b)], writes=[("sst", hp)])
                        else:
                            stt(sst[:, hp, :], sst[:, hp, :], dec[:, hp, n:n + 1], banks[b][:, 0:256], ALU.mult, ALU.add,
                                reads=[("ps", b), ("sst", hp), ("dec", hp)], writes=[("sst", hp)])
                        cp("act", sbf[n % 2][:, hp, :], sst[:, hp, :], reads=[("sst", hp)], writes=[("sbf", n % 2, hp)])
                if half == 1:
                    qr = t % 2
                    osb = etmp[qr]
                    for i in range(2):
                        cp("act", osb[:].rearrange("p (hp i c) -> p hp i c", hp=2, i=2)[:, :, i, :],
                           banks[B_BK[i]][:, 0:256].rearrange("p (hp c) -> p hp c", hp=2),
                           reads=[("ps", B_BK[i])], writes=[("etmp", qr)])
                    tt("dve", osb[:], osb[:], banks[A_BK][:, :], ALU.add,
                       reads=[("ps", A_BK), ("etmp", qr)], writes=[("etmp", qr)])
                    act(sqt[qr][:], osb[:], AF.Square, reads=[("etmp", qr)], writes=[("sqt", qr)])
                    reduce_x(ssq[:, t, :], sqt[qr][:].rearrange("p (a b) -> p a b", b=128),
                             reads=[("sqt", qr)], writes=[("ssq", t)])
                    ts("dve", msq[:, t, :], ssq[:, t, :], 1.0 / 128, EPS, ALU.mult, ALU.add,
                       reads=[("ssq", t)], writes=[("msq", t)])
                    tt("pool", rsq[:, t, :], msq[:, t, :], neghalf[:, 0:4], ALU.pow,
                       reads=[("msq", t), "neghalf"], writes=[("rsq", t)])
                    tt("pool", osb[:].rearrange("p (a b) -> p a b", b=128), osb[:].rearrange("p (a b) -> p a b", b=128),
                       rsq[:, t, :].unsqueeze(2).to_broadcast([128, 4, 128]), ALU.mult,
                       reads=[("etmp", qr), ("rsq", t)], writes=[("etmp", qr)])
                    tt("pool", ytm[qr][:], osb[:], gsog[:, t, :], ALU.mult,
                       reads=[("etmp", qr), ("gsog", t)], writes=[("ytm", qr)])
                    b = prot.next()
                    for h in range(4):
                        tr(bk_bf(b)[:, h * 128:(h + 1) * 128], ytm[qr][:, h * 128:(h + 1) * 128],
                           reads=[("ytm", qr)], writes=[("ps", b)])
                    cp("act", yTg[:, :, t * 128:(t + 1) * 128],
                       bk_bf(b)[:, 0:512].rearrange("p (a b) -> p a b", b=128),
                       reads=[("ps", b)], writes=[("yTg", t)])
        S.barrier()
        es2.close()
        if stop == "gla":
            dump("yTg", yTg[:], [128, 4, S_LEN], BF16)

    if lvl >= 2:
        es_f = ExitStack()
        yTf = mk_alloc(es_f)("yTf", [128, 4, S_LEN], BF16)
        cp3t = mk_alloc(es_f)("cp3t", [128, 3, S_LEN], BF16)
        es3 = ExitStack()
        sb3 = mk_alloc(es3)
        QTs = [aview(0, 128, [2, S_LEN], BF16), aview(8192, 128, [2, S_LEN], BF16)]
        KTs = [aview(16384, 128, [2, S_LEN], BF16), aview(24576, 128, [2, S_LEN], BF16)]
        vf = aview(32768, 128, [NT, 8, 128], BF16)
        spf = sb3("spf", [8, S_LEN], F32)[:]
        cp3 = cp3t[0:8, :, :]
        cpart = [cp3[:, i, :] for i in range(3)]
        gqt = sb3("gqt", [64, 1], F32)
        gq8 = sb3("gq8", [64, 1], F32)
        gkt = sb3("gkt", [64, 1], F32)
        ss8 = sb3("ss8", [128, 2, NT, 4], F32)
        ms8 = sb3("ms8", [128, 2, NT, 4], F32)
        rs8 = sb3("rs8", [128, 2, NT, 4], F32)
        dma("sp", gqt[:], dr["gq"], writes=["gqt"])
        dma("sp", gkt[:], dr["gk"], writes=["gkt"])
        QTf = aview(0, 128, [4 * S_LEN], BF16)
        KTf = aview(16384, 128, [4 * S_LEN], BF16)
        ts("dve", gq8[:], gqt[:], 0.125, None, ALU.mult, reads=["gqt"], writes=["gq8"])
        prot = _Rot(list(range(8)))
        vfk = [("vf", t) for t in range(NT)]
        memset("pool", aview(32768, 128, [NT * 8 * 128], BF16), 0.0, writes=vfk)
        vf4 = aview(32768, 128, [NT * 4, 2, 128], BF16)
        memset("pool", vf4[:, :, 0, 64:65], 1.0, writes=vfk)
        memset("pool", vf4[:, :, 1, 63:64], 1.0, writes=vfk)

        bft = sb3("bft", [8, 1], F32)
        nbf = sb3("nbf", [8, 1], F32)
        tmp8 = [sb3(f"tmp8{i}", [8, 512], F32) for i in range(1)]
        cneg = sb3("cneg", [8, S_LEN], F32)[:]
        dma("sp", bft[:], dr["b_f"], writes=["bft"])
        ts("dve", nbf[:], bft[:], -1.0, None, ALU.mult, reads=["bft"], writes=["nbf"])
        for tb in range(NB):
            tbs = slice(tb * 512, (tb + 1) * 512)
            b = prot.next()
            for c in range(KC):
                mm(banks[b][0:8, :], wf[:, c, :], hT[:, c, tbs], c == 0, c == KC - 1,
                   reads=["wf"], writes=[("ps", b)])
            r = 0
            act(tmp8[r][:], banks[b][0:8, :], AF.Exp, reads=[("ps", b), "nbf"], writes=[("tmp8", r)],
                scale=-1.0, bias=nbf[:, 0:1])
            act(spf[:, tbs], tmp8[r][:], AF.Ln, reads=[("tmp8", r)], writes=["spf"], bias=1.0)
        S.add("dve", lambda e: e.tensor_tensor_scan(out=cneg, data0=spf, data1=spf, initial=0.0,
                                                    op0=ALU.add, op1=ALU.max),
              reads=["spf"], writes=["cneg"], dur=4400.0)
        cr = cneg
        cp("dve", cpart[0], cneg, reads=["cneg"], writes=[("cpart", 0)])
        tt("dve", cr, cneg, cpart[0], ALU.subtract, reads=["cneg", ("cpart", 0)], writes=["cneg"])
        cp("dve", cpart[1], cr, reads=["cneg"], writes=[("cpart", 1)])
        tt("dve", cr, cr, cpart[1], ALU.subtract, reads=["cneg", ("cpart", 1)], writes=["cneg"])
        cp("dve", cpart[2], cr, reads=["cneg"], writes=[("cpart", 2)])
        memset("dve", QTf[64:96, :], 0.0, writes=["QTpad"])
        memset("pool", QTf[96:128, :], 1.0, writes=["QTpad"])
        memset("dve", KTf[64:96, :], -1.0, writes=["KTpad"])
        memset("pool", KTf[96:128, :], 0.0, writes=["KTpad"])

        sq2 = [sb3(f"sq2{i}", [128, 256], F32) for i in range(3)]
        qkn = [sb3(f"qkn{i}", [128, 256], BF16) for i in range(3)]
        qkraw = [sb3(f"qkraw{i}", [128, 256], F32) for i in range(3)]
        pt = [sb3(f"pt{i}", [128, 512], BF16) for i in range(4)]
        NY = 4
        yun = [sb3(f"yun{i}", [128, 512], F32) for i in range(NY)]
        rs4 = [sb3(f"rs4{i}", [128, 4], F32) for i in range(NY)]
        rc4 = [sb3(f"rc4{i}", [128, 4], F32) for i in range(NY)]
        rrt = sb3("rrt", [NY, 512], F32)
        rr = [rrt[i:i + 1, :] for i in range(NY)]
        rbc = [sb3(f"rbc{i}", [128, 512], F32) for i in range(3)]

        obanks = _Rot([0, 1])
        sbanks = _Rot([2, 3, 4, 5, 6, 7])
        ptrot = _Rot(list(range(4)))

        for t in range(NT):
            tsl = slice(t * 128, (t + 1) * 128)
            bv = sbanks.next()
            for c in range(KC):
                mm(banks[bv][:, :], hT[:, c, tsl], wv[:, c, :], c == 0, c == KC - 1,
                   reads=[("wv", c)], writes=[("ps", bv)])
            for par in range(2):
                cp("dve", vf[:, t, :, :].rearrange("p (pr two) c -> p pr two c", two=2)[:, :, par, par * 64:(par + 1) * 64],
                   banks[bv][:, :].rearrange("p (pr two d) -> p pr two d", two=2, d=64)[:, :, par, :],
                   reads=[("ps", bv)], writes=[("vf", t)])

        qk_count = [0]

        def proj_tile(pr, t):
            sl = pr % 2
            tsl = slice(t * 128, (t + 1) * 128)
            bqk = sbanks.next()
            for c in range(KC):
                mm(banks[bqk][:, 0:256], hT[:, c, tsl], wqk[sl][:, c, :], c == 0, c == KC - 1,
                   reads=[("wqk", sl, c)], writes=[("ps", bqk)])
            qr = qk_count[0] % 3
            qk_count[0] += 1
            cp("act", qkraw[qr][:], banks[bqk][:, 0:256], reads=[("ps", bqk)], writes=[("qkraw", qr)])
            tt("dve", sq2[qr][:], qkraw[qr][:], qkraw[qr][:], ALU.mult, reads=[("qkraw", qr)], writes=[("sq2", qr)])
            reduce_x(ss8[:, sl, t, :], sq2[qr][:].rearrange("p (a b) -> p a b", b=64),
                     reads=[("sq2", qr)], writes=[("ss8", sl, t)])
            ts("dve", ms8[:, sl, t, :], ss8[:, sl, t, :], 1.0 / 64, EPS, ALU.mult, ALU.add,
               reads=[("ss8", sl, t)], writes=[("ms8", sl, t)])
            tt("pool", rs8[:, sl, t, :], ms8[:, sl, t, :], neghalf[:, 0:4], ALU.pow,
               reads=[("ms8", sl, t), "neghalf"], writes=[("rs8", sl, t)])
            tt("dve", qkn[qr][:].rearrange("p (a b) -> p a b", b=64),
               qkraw[qr][:].rearrange("p (a b) -> p a b", b=64),
               rs8[:, sl, t, :].unsqueeze(2).to_broadcast([128, 4, 64]), ALU.mult,
               reads=[("qkraw", qr), ("rs8", sl, t)], writes=[("qkn", qr)])
            bt = sbanks.next()
            for i in range(4):
                tr(bk_bf(bt)[0:64, i * 128:(i + 1) * 128], qkn[qr][:, i * 64:(i + 1) * 64],
                   reads=[("qkn", qr)], writes=[("ps", bt)])
            act(QTs[sl][0:64, :, tsl], bk_bf(bt)[0:64, 0:256].rearrange("p (a b) -> p a b", b=128), AF.Copy,
                reads=[("ps", bt), "gq8"], writes=[("QT", sl, t)], scale=gq8[:, 0:1])
            act(KTs[sl][0:64, :, tsl], bk_bf(bt)[0:64, 256:512].rearrange("p (a b) -> p a b", b=128), AF.Copy,
                reads=[("ps", bt), "gkt"], writes=[("KT", sl, t)], scale=gkt[:, 0:1])

        def pair_setup(pr):
            sl = pr % 2
            if pr >= 1:
                for c in range(KC):
                    dma("pool", wqk[sl][:, c, :], dr["w_fqk"][pr, :, c, :], writes=[("wqk", sl, c)])
            for hh2 in range(2):
                h = 2 * pr + hh2
                dma("sp", QTs[sl][64:67, hh2, :], cp3[h:h + 1, :, :],
                    reads=[("cpart", 0), ("cpart", 1), ("cpart", 2), "QTpad"], writes=[("QTa", sl, hh2)])
                dma("sp", KTs[sl][96:99, hh2, :], cp3[h:h + 1, :, :],
                    reads=[("cpart", 0), ("cpart", 1), ("cpart", 2), "KTpad"], writes=[("KTa", sl, hh2)])

        unit_count = [0]

        def attn_unit(pr, hh2, qb):
            sl = pr % 2
            h = 2 * pr + hh2
            QT, KT = QTs[sl], KTs[sl]
            bo = obanks.next()
            nk = 4 * qb + 4
            qdeps = [("QT", sl, t) for t in range(qb * 4, qb * 4 + 4)] + [("QTa", sl, hh2), "QTpad", "KTpad"]
            for kt in range(nk):
                j = kt - 4 * qb
                c0 = 128 * j if j > 0 else 0
                N = 512 - c0
                bs = sbanks.next()
                mm(banks[bs][:, 0:N], KT[:, hh2, kt * 128:(kt + 1) * 128],
                   QT[:, hh2, qb * 512 + c0:(qb + 1) * 512], True, j < 0,
                   reads=[("KT", sl, kt), ("KTa", sl, hh2)] + qdeps, writes=[("ps", bs)])
                if j >= 0:
                    mm(banks[bs][:, 0:128], negib[:], trib[:], False, True,
                       reads=["negi", "tri"], writes=[("ps", bs)])
                pr_ = ptrot.next()
                act(pt[pr_][:, 0:N], banks[bs][:, 0:N], AF.Exp, reads=[("ps", bs)], writes=[("pt", pr_)])
                mm(banks[bo][:, c0:512], vf[:, kt, h, :], pt[pr_][:, 0:N], kt == 0, kt == nk - 1,
                   reads=[("vf", kt), ("pt", pr_)], writes=[("ps", bo)])
            u = unit_count[0]
            unit_count[0] += 1
            yr = u % NY
            par = h % 2
            ev = slice(0, 65) if par == 0 else slice(0, 128)
            rrow = 64 if par == 0 else 63
            ys = slice(par * 64, (par + 1) * 64)
            cp("dve", yun[yr][ev, :], banks[bo][ev, :], reads=[("ps", bo)], writes=[("yun", yr)])
            dma("sp", rs4[yr][:], yun[yr][rrow:rrow + 1, :], reads=[("yun", yr)], writes=[("rs4", yr)])
            recip(rc4[yr][:], rs4[yr][:], reads=[("rs4", yr)], writes=[("rc4", yr)])
            dma("sp", rr[yr], rc4[yr][:], reads=[("rc4", yr)], writes=[("rr", yr)])
            br = u % 3
            dma("sp", rbc[br][ys, :], rr[yr].unsqueeze(1).to_broadcast([1, 64, 512]), reads=[("rr", yr)], writes=[("rbc", br)])
            tt("dve", yTf[ys, h // 2, qb * 512:(qb + 1) * 512], yun[yr][ys, :], rbc[br][ys, :], ALU.mult,
               reads=[("yun", yr), ("rbc", br)], writes=[("yTf", h, qb)])

        pair_setup(0)
        for t in range(NT):
            proj_tile(0, t)
        for pr in range(4):
            if pr + 1 < 4:
                pair_setup(pr + 1)
            if pr == 3 and lvl >= 3:
                wog = wv[:].rearrange("p a b -> p (a b)")[:, 0:4 * D].rearrange("p (h d) -> p h d", d=D)
                wof = cp3t[:].rearrange("p a b -> p (a b)")[:, 0:4 * D].rearrange("p (h d) -> p h d", d=D)
                for h in range(4):
                    dma("pool", wog[:, h, :], dr["wo_g"][:, h, :], writes=[("wog", h)] + [("wv", c) for c in range(KC)])
                for h in range(4):
                    dma("pool", wof[:, h, :], dr["wo_f"][:, h, :], writes=[("wof", h)] + [("cpart", j_) for j_ in range(3)])
            units = [(hh2, qb) for hh2 in range(2) for qb in range(NB)]
            for ui, (hh2, qb) in enumerate(units):
                attn_unit(pr, hh2, qb)
                if pr + 1 < 4:
                    proj_tile(pr + 1, 2 * ui)
                    proj_tile(pr + 1, 2 * ui + 1)
        S.barrier()
        es3.close()
        if stop == "fox":
            dump("yTf", yTf[:], [128, 4, S_LEN], BF16)

    x1 = arena[:].rearrange("p (t d) -> p t d", d=D)

    if lvl >= 3:
        es_b = ExitStack()
        sbb = mk_alloc(es_b)
        for t in range(NT):
            dma("sp", x1[:, t, :], dr["x"][t * 128:(t + 1) * 128, :], writes=[("x1", t)])
        htm2 = [(sbb(f"htmc{i}", [128, D], BF16)[:], ("htm", i)) for i in range(8)]
        prot = _Rot(list(range(8)))
        for t in range(NT):
            tsl = slice(t * 128, (t + 1) * 128)
            for nh in range(2):
                nsl = slice(nh * 512, (nh + 1) * 512)
                b = prot.next()
                for h in range(4):
                    mm(banks[b][:, :], yTg[:, h, tsl], wog[:, h, nsl], h == 0, False,
                       reads=[("wog", h)], writes=[("ps", b)])
                for h in range(4):
                    mm(banks[b][:, :], yTf[:, h, tsl], wof[:, h, nsl], False, h == 3,
                       reads=[("wof", h)], writes=[("ps", b)])
                tt("dve", x1[:, t, nsl], x1[:, t, nsl], banks[b][:, :], ALU.add,
                   reads=[("ps", b), ("x1", t)], writes=[("x1", t)])
        if lvl >= 4:
            norm_to_hT(lambda t: (x1[:, t, :], [("x1", t)]), g2t, "g2", hT, "hT", "n2", htm2)
        S.barrier()
        es_b.close()
    if lvl >= 2:
        es_f.close()
    es_mix.close()
    if stop == "wo":
        dump("x1", arena[:], [128, 16384], F32)

    if lvl >= 4:
        es_c = ExitStack()
        sbc = mk_alloc(es_c)
        h2T = hT
        htm3 = [(sbc(f"htmd{i}", [128, D], BF16)[:], ("htm", i)) for i in range(8)]
        gbuf = sbc("gbuf", [128, GMAX, S_LEN], BF16)
        wu = [sbc(f"wu{i}", [128, KC, 256], BF16) for i in range(3)]
        wd = sbc("wd", [128, GMAX, D], BF16)
        cwg = sbc("s_cwg", [128, NJ, 4], F32)
        cwv = sbc("s_cwv", [128, NJ, 4], F32)
        ug = [sbc(f"ug{i}", [128, 1026], F32) for i in range(2)]
        uv = [sbc(f"uv{i}", [128, 1026], F32) for i in range(2)]
        cg = [sbc(f"cg{i}", [128, 512], F32) for i in range(2)]
        cv = [sbc(f"cv{i}", [128, 512], F32) for i in range(2)]
        sg = [sbc(f"sg{i}", [128, 512], F32) for i in range(2)]
        ptmp = sg
        dma("sp", cwg[:], dr["cwg"], writes=["cwg"])
        dma("sp", cwv[:], dr["cwv"], writes=["cwv"])
        for j in range(min(2, NJ)):
            dma("pool", wu[j % 3][:], dr["w_up"][j], writes=[("wu", j % 3)])
        for j in range(FFN_GROUPS[0][0], FFN_GROUPS[0][1]):
            dma("pool", wd[:, j, :], dr["w_down"][j], writes=[("wd", j)])
        prot = _Rot(list(range(8)))
        urot = 0
        crot = 0
        WU_AHEAD = 2
        for (j0, j1) in FFN_GROUPS:
            for j in range(j0, j1):
                if j0 > 0:
                    dma("pool", wd[:, j - j0, :], dr["w_down"][j], writes=[("wd", j - j0)])
            for j in range(j0, j1):
                jj = j - j0
                if j + WU_AHEAD < NJ:
                    dma("pool", wu[(j + WU_AHEAD) % 3][:], dr["w_up"][j + WU_AHEAD], writes=[("wu", (j + WU_AHEAD) % 3)])
                w = wu[j % 3]
                for half in range(2):
                    ur = urot % 2
                    urot += 1
                    bufs = ((ug[ur], ("ug", ur), ug[1 - ur], ("ug", 1 - ur), 0, cwg, "cwg"),
                            (uv[ur], ("uv", ur), uv[1 - ur], ("uv", 1 - ur), 128, cwv, "cwv"))
                    for (ub, ubk, uprev, uprevk, wo_, cw, cwk) in bufs:
                        if half == 0:
                            memset("pool", ub[:, 0:2], 0.0, writes=[ubk])
                        else:
                            cp("pool", ub[:, 0:2], uprev[:, 1024:1026], reads=[uprevk], writes=[ubk])
                        for sub in range(2):
                            tb = half * 2 + sub
                            tbs = slice(tb * 512, (tb + 1) * 512)
                            b = prot.next()
                            for c in range(KC):
                                mm(banks[b][:, :], w[:, c, wo_:wo_ + 128], h2T[:, c, tbs], c == 0, c == KC - 1,
                                   reads=[("wu", j % 3), ("hT", c, tb)], writes=[("ps", b)])
                            cp("act", ub[:, 2 + sub * 512:2 + (sub + 1) * 512], banks[b][:, :],
                               reads=[("ps", b)], writes=[ubk])
                    for sub in range(2):
                        tb = half * 2 + sub
                        tbs = slice(tb * 512, (tb + 1) * 512)
                        cr_ = crot % 2
                        crot += 1
                        outs = []
                        for (ub, ubk, uprev, uprevk, wo_, cw, cwk), cb, cbk in ((bufs[0], cg[cr_], ("cg", cr_)), (bufs[1], cv[cr_], ("cv", cr_))):
                            o = sub * 512
                            if wo_ == 0:
                                act(cb[:], ub[:, o + 2:o + 514], AF.Identity, reads=[ubk, cwk], writes=[cbk],
                                    scale=cw[:, j, 2:3], bias=cw[:, j, 3:4])
                            else:
                                ts("pool", cb[:], ub[:, o + 2:o + 514], cw[:, j, 2:3], cw[:, j, 3:4], ALU.mult, ALU.add,
                                   reads=[ubk, cwk], writes=[cbk])
                            stt(cb[:], ub[:, o + 1:o + 513], cw[:, j, 1:2], cb[:], ALU.mult, ALU.add,
                                reads=[ubk, cwk, cbk], writes=[cbk])
                            stt(cb[:], ub[:, o:o + 512], cw[:, j, 0:1], cb[:], ALU.mult, ALU.add,
                                reads=[ubk, cwk, cbk], writes=[cbk])
                        act(sg[cr_][:], cg[cr_][:], AF.Silu, reads=[("cg", cr_)], writes=[("sg", cr_)])
                        tt("pool", gbuf[:, jj, tbs], sg[cr_][:], cv[cr_][:], ALU.mult,
                           reads=[("sg", cr_), ("cv", cr_)], writes=[("gbuf", jj, tb)])
            ng = j1 - j0
            for t in range(NT):
                tsl = slice(t * 128, (t + 1) * 128)
                for nh in range(2):
                    nsl = slice(nh * 512, (nh + 1) * 512)
                    b = prot.next()
                    for jj in range(ng):
                        mm(banks[b][:, :], gbuf[:, jj, tsl], wd[:, jj, nsl], jj == 0, jj == ng - 1,
                           reads=[("gbuf", jj, t // 4), ("wd", jj)], writes=[("ps", b)])
                    tt("dve", x1[:, t, nsl], x1[:, t, nsl], banks[b][:, :], ALU.add,
                       reads=[("ps", b), ("x1", t)], writes=[("x1", t)])
        if lvl >= 5:
            norm_to_hT(lambda t: (x1[:, t, :], [("x1", t)]), g3t, "g3", hT, "hT", "n3", htm3)
        S.barrier()
        es_c.close()
        if stop == "ffn":
            dump("x2", arena[:], [128, 16384], F32)

    if lvl >= 5:
        es_d = ExitStack()
        sbd = mk_alloc(es_d)
        h3T = hT
        wpg = sbd("wpg", [128, KC, D], BF16)
        wpe = sbd("wpe", [128, 2, D], BF16)
        pb = sbd("pb", [128, NT, 256], BF16)
        pT = sbd("pT", [128, 2, S_LEN], BF16)
        bpg = sbd("bpg", [128, D], F32)
        peg = sbd("peg", [128, D], F32)
        ess = sbd("ess", [128, NT, 2], F32)
        ems = sbd("ems", [128, NT], F32)
        ers = sbd("ers", [128, NT], F32)
        gpre = [sbd(f"gpre{i}", [128, 512], F32) for i in range(2)]
        sig = [sbd(f"sig{i}", [128, 512], F32) for i in range(2)]
        en = [sbd(f"en{i}", [128, 512], F32) for i in range(2)]
        ot = [sbd(f"ot{i}", [128, D], F32) for i in range(2)]
        for c in range(KC):
            dma("pool", wpg[:, c, :], dr["w_pg"][:, c, :], writes=[("wpg", c)])
        for c in range(2):
            dma("pool", wpe[:, c, :], dr["w_pe"][:, c, :], writes=[("wpe", c)])
        for t in range(NT):
            dma("pool", pb[:, t, :], dr["p"][t * 128:(t + 1) * 128, :], writes=[("pb", t)])
        dma("sp", bpg[:], dr["bpg_bc"], writes=["bpg"])
        dma("sp", peg[:], dr["peg_bc"], writes=["peg"])
        prot = _Rot(list(range(8)))
        for tb in range(NB):
            for kc in range(2):
                b = prot.next()
                for ti in range(4):
                    t = tb * 4 + ti
                    tr(bk_bf(b)[:, ti * 128:(ti + 1) * 128], pb[:, t, kc * 128:(kc + 1) * 128],
                       reads=[("pb", t)], writes=[("ps", b)])
                cp("act", pT[:, kc, tb * 512:(tb + 1) * 512], bk_bf(b)[:, 0:512], reads=[("ps", b)], writes=[("pT", kc, tb)])
        perot = _Rot([0, 1, 2, 3])
        grot = _Rot([4, 5, 6, 7])
        out_ops = []
        k = 0
        for t in range(NT):
            tsl = slice(t * 128, (t + 1) * 128)
            peb = []
            for nh in range(2):
                nsl = slice(nh * 512, (nh + 1) * 512)
                b = perot.next()
                peb.append(b)
                for kc in range(2):
                    mm(banks[b][:, :], pT[:, kc, tsl], wpe[:, kc, nsl], kc == 0, kc == 1,
                       reads=[("pT", kc, t // 4), ("wpe", kc)], writes=[("ps", b)])
                act(junk[:, 0:512], banks[b][:, :], AF.Square, reads=[("ps", b)], writes=[("ess", t, nh), "junk"],
                    accum_out=ess[:, t, nh:nh + 1])
            tt("dve", ems[:, t:t + 1], ess[:, t, 0:1], ess[:, t, 1:2], ALU.add,
               reads=[("ess", t, 0), ("ess", t, 1)], writes=[("ems", t)])
            ts("dve", ems[:, t:t + 1], ems[:, t:t + 1], 1.0 / D, EPS, ALU.mult, ALU.add,
               reads=[("ems", t)], writes=[("ems", t)])
            tt("pool", ers[:, t:t + 1], ems[:, t:t + 1], neghalf[:, 0:1], ALU.pow,
               reads=[("ems", t), "neghalf"], writes=[("ers", t)])
            orr = t % 2
            for nh in range(2):
                nsl = slice(nh * 512, (nh + 1) * 512)
                b = grot.next()
                for c in range(KC):
                    mm(banks[b][:, :], h3T[:, c, tsl], wpg[:, c, nsl], c == 0, c == KC - 1,
                       reads=[("wpg", c)], writes=[("ps", b)])
                r = k % 2
                k += 1
                tt("dve", gpre[r][:], banks[b][:, :], bpg[:, nsl], ALU.add,
                   reads=[("ps", b), "bpg"], writes=[("gpre", r)])
                act(sig[r][:], gpre[r][:], AF.Sigmoid, reads=[("gpre", r)], writes=[("sig", r)])
                stt(en[r][:], banks[peb[nh]][:, :], ers[:, t:t + 1], peg[:, nsl], ALU.mult, ALU.mult,
                    reads=[("ps", peb[nh]), ("ers", t), "peg"], writes=[("en", r)])
                tt("pool", en[r][:], en[r][:], sig[r][:], ALU.mult,
                   reads=[("en", r), ("sig", r)], writes=[("en", r)])
                tt("dve", ot[orr][:, nsl], en[r][:], x1[:, t, nsl], ALU.add,
                   reads=[("en", r), ("x1", t)], writes=[("ot", orr)])
            out_ops.append(dma("sp", out_d[t * 128:(t + 1) * 128, :], ot[orr][:], reads=[("ot", orr)]))
        S.barrier()
        es_d.close()

    S.barrier()
    S.finalize()
    with nc.Block() as block:
        @block.sync
        def _(e):
            S.emit_engine("sp", e, sems, dsems)
            last = {}
            for d in S.ops["sp"]:
                if d.dma:
                    last[d.dsem] = d
            for d in last.values():
                e.wait_ge(dsems["sp"][d.dsem], 16 * d.dgen)

        @block.scalar
        def _(e):
            S.emit_engine("act", e, sems, dsems)

        @block.vector
        def _(e):
            S.emit_engine("dve", e, sems, dsems)

        @block.gpsimd
        def _(e):
            S.emit_engine("pool", e, sems, dsems)

        @block.tensor
        def _(e):
            S.emit_engine("pe", e, sems, dsems)
    es_h.close()
    es_all.close()
    return nc, dbg_d


def _consts():
    ident = np.eye(128, dtype=np.float32)
    k = np.arange(128)
    tri = (k[:, None] > k[None, :]).astype(np.float32)
    negi = (-30000.0 * np.eye(128)).astype(np.float32)
    s_ = (np.arange(128) % 64)[:, None]
    c_ = (np.arange(256) % 64)[None, :]
    gmask = (s_ <= c_).astype(np.float32)
    return dict(ident=ident, tri=tri, negi=negi, gmask=gmask)


def _layout_shared(inp):
    f = lambda a: np.ascontiguousarray(np.asarray(a, dtype=np.float32))
    w_in = f(inp["w_in"])[0]
    w_in_l = w_in.reshape(8, 128, 3096).transpose(1, 0, 2)
    sh = {}
    sh["g1"] = f(f(inp["norm1_g"])[0].reshape(8, 128).T)
    sh["g2"] = f(f(inp["norm2_g"])[0].reshape(8, 128).T)
    sh["g3"] = f(f(inp["norm3_g"])[0].reshape(8, 128).T)
    sh["w_gla"] = f(w_in_l[:, :, 0:1552])
    wqk_ = []
    for pr in range(4):
        q = w_in_l[:, :, 1552 + pr * 128:1552 + (pr + 1) * 128]
        k = w_in_l[:, :, 2064 + pr * 128:2064 + (pr + 1) * 128]
        wqk_.append(np.concatenate([q, k], axis=2))
    sh["w_fqk"] = f(np.stack(wqk_, 0))
    sh["w_fv"] = f(w_in_l[:, :, 2576:3088])
    sh["w_f"] = f(w_in_l[:, :, 3088:3096])
    sh["lr_w"] = f(f(inp["gla_lr_w"])[0])
    sh["lr_b"] = f(f(inp["gla_lr_b"])[0].reshape(2, 128).T)
    sh["gon_bc"] = f(np.broadcast_to(np.tile(f(inp["gla_onorm_g"])[0], 4)[None, :], (128, 512)))
    sh["b_f"] = f(f(inp["fox_b_f"])[0].reshape(8, 1))
    sh["gq"] = f(f(inp["fox_qnorm_g"])[0].reshape(64, 1))
    sh["gk"] = f(f(inp["fox_knorm_g"])[0].reshape(64, 1))
    w_o = f(inp["w_o"])[0]
    sh["wo_g"] = f(w_o[:512].reshape(4, 128, 1024).transpose(1, 0, 2))
    sh["wo_f"] = f(w_o[512:].reshape(4, 128, 1024).transpose(1, 0, 2))
    w_up = f(inp["w_up"])[0].reshape(8, 128, 2 * DFF)
    blocks = []
    for j in range(NJ):
        g_ = w_up[:, :, j * 128:(j + 1) * 128]
        v_ = w_up[:, :, DFF + j * 128:DFF + (j + 1) * 128]
        blocks.append(np.concatenate([g_, v_], axis=2).transpose(1, 0, 2))
    sh["w_up"] = f(np.stack(blocks, 0))
    conv_w = f(inp["conv_w"])[0]
    conv_b = f(inp["conv_b"])[0]
    cw = np.concatenate([conv_w, conv_b[None, :]], axis=0)
    sh["cwg"] = f(cw[:, :DFF].reshape(4, NJ, 128).transpose(2, 1, 0))
    sh["cwv"] = f(cw[:, DFF:].reshape(4, NJ, 128).transpose(2, 1, 0))
    sh["w_down"] = f(f(inp["w_down"])[0].reshape(NJ, 128, 1024))
    sh["w_pe"] = f(f(inp["w_pe"])[0].reshape(2, 128, 1024).transpose(1, 0, 2))
    sh["w_pg"] = f(f(inp["w_pg"])[0].reshape(8, 128, 1024).transpose(1, 0, 2))
    sh["bpg_bc"] = f(np.broadcast_to(f(inp["b_pg"])[0][None, :], (128, 1024)))
    sh["peg_bc"] = f(np.broadcast_to(f(inp["pe_norm_g"])[0][None, :], (128, 1024)))
    sh.update(_consts())
    return sh


_CACHE = {}


def _get_program(stop="full"):
    if stop not in _CACHE:
        _CACHE[stop] = build_program(stop)
    return _CACHE[stop]


def run(inputs, stop="full", cores=NCORES):
    nc, dbg = _get_program(stop)
    sh = _layout_shared(inputs)
    x = np.asarray(inputs["x"], dtype=np.float32)
    p = np.asarray(inputs["p"], dtype=np.float32)[0]
    in_maps = []
    for b in range(cores):
        m = dict(sh)
        m["x"] = np.ascontiguousarray(x[b])
        m["p"] = np.ascontiguousarray(p[b])
        in_maps.append(m)
    res = run_bass_kernel_spmd(nc, in_maps, core_ids=list(range(cores)))
    return res


def kernel(**inputs):
    res = run(inputs, "full", NCORES)
    out = np.stack([np.asarray(r["out"], dtype=np.float32) for r in res.results], axis=0)
    return out
```

```python
import math
from contextlib import ExitStack

import numpy as np
import concourse.bass as bass
import concourse.mybir as mybir
from concourse.bass_utils import run_bass_kernel_spmd

F32 = mybir.dt.float32
BF16 = mybir.dt.bfloat16
AF = mybir.ActivationFunctionType
ALU = mybir.AluOpType
AX = mybir.AxisListType

S_LEN = 2048
D = 1024
NT = 16
NB = 4
KC = 8
EPS = 1e-6
DFF = 2816
NJ = 22
NCORES = 8

ENGINES = ("pe", "act", "dve", "pool", "sp")
NDMA_SEM = 12
SCHED_WINDOW = 200
SEM_LAT = 600.0
REORDER = True


class _Op:
    __slots__ = ("eng", "emit", "alldeps", "deps", "sig", "cnt", "idx", "dma", "dsem", "dgen",
                 "prev_dma", "dur", "seg", "pidx")

    def __init__(self, eng, emit, dma, dur):
        self.eng = eng
        self.emit = emit
        self.alldeps = ()
        self.deps = []
        self.sig = False
        self.cnt = 0
        self.idx = 0
        self.dma = dma
        self.dsem = None
        self.dgen = 0
        self.prev_dma = None
        self.dur = dur
        self.seg = 0
        self.pidx = 0


class Sched:
    def __init__(self, same_engine_sync=True):
        self.prog = []
        self.ops = {e: [] for e in ENGINES}
        self.last_w = {}
        self.readers = {}
        self.seg = 0
        self.same_engine_sync = same_engine_sync

    def add(self, eng, emit, reads=(), writes=(), dma=False, dur=100.0):
        op = _Op(eng, emit, dma, dur)
        op.seg = self.seg
        op.pidx = len(self.prog)
        deps = set()
        for r in reads:
            w = self.last_w.get(r)
            if w is not None:
                deps.add(w)
        for r in writes:
            w = self.last_w.get(r)
            if w is not None:
                deps.add(w)
            for rd in self.readers.get(r, ()):
                deps.add(rd)
        deps.discard(op)
        for r in reads:
            self.readers.setdefault(r, []).append(op)
        for r in writes:
            self.last_w[r] = op
            self.readers[r] = []
        op.alldeps = tuple(deps)
        self.prog.append(op)
        return op

    def barrier(self):
        self.seg += 1
        self.last_w = {}
        self.readers = {}

    def _schedule(self, ops):
        pend = {e: [] for e in ENGINES}
        for op in ops:
            pend[op.eng].append(op)
        order = {e: [] for e in ENGINES}
        if not REORDER:
            return pend
        inseg = set(ops)
        succ = {}
        for op in ops:
            for d in op.alldeps:
                if d in inseg:
                    succ.setdefault(d, []).append(op)
        blev = {}
        for op in reversed(ops):
            m = 0.0
            for s_ in succ.get(op, ()):
                lat = SEM_LAT if (s_.eng != op.eng or op.dma) else 0.0
                v = blev[s_] + lat
                if v > m:
                    m = v
            blev[op] = op.dur + m
        fin = {}
        free = {e: 0.0 for e in ENGINES}
        left = len(ops)
        W = SCHED_WINDOW
        while left:
            best = None
            for e in ENGINES:
                lst = pend[e]
                if not lst:
                    continue
                ef = free[e]
                cand = None
                for k in range(min(W, len(lst))):
                    op = lst[k]
                    rt = 0.0
                    ok = True
                    for d in op.alldeps:
                        if d not in inseg:
                            continue
                        ft = fin.get(d)
                        if ft is None:
                            ok = False
                            break
                        if d.eng != e or d.dma:
                            ft += SEM_LAT
                        if ft > rt:
                            rt = ft
                    if not ok:
                        continue
                    st = rt if rt > ef else ef
                    key = (st, -blev[op], k)
                    if cand is None or key < cand[0]:
                        cand = (key, op)
                if cand is not None and (best is None or cand[0] < best[0]):
                    best = (cand[0], e, cand[1])
            (st, _nb, k), e, op = best
            pend[e].pop(k)
            if op.dma:
                issue = 650.0 if e == "pool" else 100.0
                fin[op] = st + issue + op.dur
                free[e] = st + issue
            else:
                fin[op] = st + op.dur
                free[e] = st + op.dur
            order[e].append(op)
            left -= 1
        self.est_time = getattr(self, "est_time", 0.0) + max(list(fin.values()) + [0.0])
        return order

    def _prune(self, eng, deps):
        best = {}
        dbest = {}
        for d in deps:
            if d.dma:
                k = (d.eng, d.dsem)
                b = dbest.get(k)
                if b is None or d.dgen > b.dgen:
                    dbest[k] = d
            else:
                if d.eng == eng and (eng == "pe" or not self.same_engine_sync):
                    continue
                b = best.get(d.eng)
                if b is None or d.idx > b.idx:
                    best[d.eng] = d
        return list(best.values()) + list(dbest.values())

    def finalize(self):
        nseg = self.seg + 1
        segs = [[] for _ in range(nseg)]
        for op in self.prog:
            segs[op.seg].append(op)
        ndma = {e: 0 for e in ENGINES}
        hist = {e: [] for e in ENGINES}
        for si, sops in enumerate(segs):
            order = self._schedule(sops)
            seg_dmas = []
            lasts = []
            for e in ENGINES:
                for op in order[e]:
                    op.idx = len(self.ops[e])
                    self.ops[e].append(op)
                    if op.dma:
                        k = ndma[e]
                        op.dsem = k % NDMA_SEM
                        op.dgen = k // NDMA_SEM + 1
                        op.prev_dma = hist[e][k - NDMA_SEM] if k >= NDMA_SEM else None
                        hist[e].append(op)
                        ndma[e] += 1
                        seg_dmas.append(op)
                if order[e]:
                    lasts.append(order[e][-1])
            for op in sops:
                op.deps = self._prune(op.eng, [d for d in op.alldeps if d.seg == op.seg])
            for e in ENGINES:
                bop = _Op(e, None, False, 0.0)
                bop.seg = si
                bop.idx = len(self.ops[e])
                bop.deps = self._prune("__none__", [d for d in lasts + seg_dmas if not (d.eng == e and not d.dma)])
                self.ops[e].append(bop)
        for e in ENGINES:
            for op in self.ops[e]:
                for d in op.deps:
                    d.sig = True
        for e in ENGINES:
            c = 0
            for op in self.ops[e]:
                if op.sig and not op.dma and op.emit is not None:
                    c += 1
                op.cnt = c

    def emit_engine(self, eng_name, eng, sems, dsems):
        waited = {}
        for op in self.ops[eng_name]:
            need = []
            for d in op.deps:
                if d.dma:
                    need.append((("d", d.eng, d.dsem), 16 * d.dgen, dsems[d.eng][d.dsem]))
                else:
                    need.append((("c", d.eng), d.cnt, sems[d.eng]))
            if op.dma and op.prev_dma is not None:
                d = op.prev_dma
                need.append((("d", d.eng, d.dsem), 16 * d.dgen, dsems[d.eng][d.dsem]))
            for key, val, sem in need:
                if waited.get(key, 0) >= val:
                    continue
                eng.wait_ge(sem, val)
                waited[key] = val
            if op.emit is None:
                continue
            ins = op.emit(eng)
            if op.dma:
                ins.then_inc(dsems[eng_name][op.dsem], 16)
            elif op.sig:
                ins.then_inc(sems[eng_name], 1)


class _Rot:
    def __init__(self, items):
        self.items = list(items)
        self.i = 0

    def next(self):
        it = self.items[self.i % len(self.items)]
        self.i += 1
        return it


def _fsize(ap):
    n = 1
    for s_ in ap.shape[1:]:
        n *= s_
    return n


FFN_GROUPS = [(0, 8), (8, 15), (15, 22)]
GMAX = 8

DRAM_INPUTS = [
    ("x", [S_LEN, D]), ("p", [S_LEN, 256]),
    ("g1", [128, 8]), ("g2", [128, 8]), ("g3", [128, 8]),
    ("w_gla", [128, 8, 1552]), ("w_fqk", [4, 128, 8, 256]), ("w_fv", [128, 8, 512]), ("w_f", [128, 8, 8]),
    ("lr_w", [16, 256]), ("lr_b", [128, 2]), ("gon_bc", [128, 512]),
    ("b_f", [8, 1]), ("gq", [64, 1]), ("gk", [64, 1]),
    ("wo_g", [128, 4, 1024]), ("wo_f", [128, 4, 1024]),
    ("w_up", [NJ, 128, 8, 256]), ("cwg", [128, NJ, 4]), ("cwv", [128, NJ, 4]),
    ("w_down", [NJ, 128, 1024]),
    ("w_pe", [128, 2, 1024]), ("w_pg", [128, 8, 1024]),
    ("bpg_bc", [128, 1024]), ("peg_bc", [128, 1024]),
    ("ident", [128, 128]), ("tri", [128, 128]), ("negi", [128, 128]), ("gmask", [128, 256]),
]

PHASES = ["hT", "gla", "fox", "wo", "ffn", "full"]
GLA_LIMIT = 6


def build_program(stop="full"):
    lvl = PHASES.index(stop)
    nc = bass.Bass("TRN2", target_bir_lowering=False)
    dr = {}
    for name, shape in DRAM_INPUTS:
        dr[name] = nc.dram_tensor(name, list(shape), F32, kind="ExternalInput").ap()
    out_d = nc.dram_tensor("out", [S_LEN, D], F32, kind="ExternalOutput").ap()
    dbg_d = {}

    S = Sched()
    es_all = ExitStack()

    def mk_alloc(es):
        def sb(name, shape, dt):
            return es.enter_context(nc.sbuf_tensor(name, list(shape), dt))
        return sb

    sbp = mk_alloc(es_all)
    banks = [es_all.enter_context(nc.psum_tensor(f"bank{i}", [128, 512], F32)) for i in range(8)]
    sems = {e: es_all.enter_context(nc.semaphore("s_" + e)) for e in ENGINES}
    dsems = {e: [es_all.enter_context(nc.semaphore(f"d_{e}{i}")) for i in range(NDMA_SEM)]
             for e in ("sp", "pool", "act")}

    def _bytes(ap):
        return ap.shape[0] * _fsize(ap) * (2 if ap.dtype == BF16 else 4)

    def dma(q, out, in_, reads=(), writes=()):
        return S.add(q, lambda e: e.dma_start(out=out, in_=in_), reads, writes, dma=True,
                     dur=5000.0 + _bytes(out) / 100.0)

    def mm(out, lhsT, rhs, start, stop, reads=(), writes=()):
        n = max(_fsize(rhs), 64)
        d = n / 2.4 * (4.0 if rhs.dtype == F32 else 1.0) + 12.0
        if lhsT.shape[0] < 128:
            d = 2.0 * d + 50.0
        return S.add("pe", lambda e: e.matmul(out, lhsT=lhsT, rhs=rhs, start=start, stop=stop), reads, writes, dur=d)

    def tr(out, in_, reads=(), writes=()):
        return S.add("pe", lambda e: e.transpose(out=out, in_=in_, identity=identb[:]),
                     list(reads) + ["ident"], writes, dur=70.0)

    def act(out, in_, func, reads=(), writes=(), scale=1.0, bias=None, accum_out=None):
        kw = {}
        if bias is not None:
            kw["bias"] = bias
        if accum_out is not None:
            kw["accum_out"] = accum_out
        d = _fsize(in_) / 1.2 + 200.0
        return S.add("act", lambda e: e.activation(out=out, in_=in_, func=func, scale=scale, **kw), reads, writes, dur=d)

    def _vdur(eng, n, mult=1.0):
        if eng == "pool":
            return n * mult / 0.55 + 220.0
        return n * mult / 0.96 + 80.0

    def _is_psum(ap):
        return "psum" in str(ap.space).lower() or "bank" in str(ap.name)

    def tt(eng, out, in0, in1, op, reads=(), writes=()):
        m = 1.0 if (eng == "pool" or _is_psum(in0) or _is_psum(in1)) else 1.8
        d = _vdur(eng, _fsize(out), m)
        if op == ALU.pow:
            d = 450.0 + 150.0 * _fsize(out)
        return S.add(eng, lambda e: e.tensor_tensor(out=out, in0=in0, in1=in1, op=op), reads, writes, dur=d)

    def ts(eng, out, in0, s1, s2, op0, op1=None, reads=(), writes=()):
        d = _vdur(eng, _fsize(out))
        if op1 is None:
            return S.add(eng, lambda e: e.tensor_scalar(out=out, in0=in0, scalar1=s1, scalar2=None, op0=op0), reads, writes, dur=d)
        return S.add(eng, lambda e: e.tensor_scalar(out=out, in0=in0, scalar1=s1, scalar2=s2, op0=op0, op1=op1), reads, writes, dur=d)

    def stt(out, in0, scalar, in1, op0, op1, reads=(), writes=()):
        m = 1.0 if (_is_psum(in0) or _is_psum(in1)) else 1.8
        return S.add("dve", lambda e: e.scalar_tensor_tensor(out=out, in0=in0, scalar=scalar, in1=in1, op0=op0, op1=op1), reads, writes,
                     dur=_vdur("dve", _fsize(out), m))

    def cp(eng, out, in_, reads=(), writes=()):
        if eng == "act":
            return act(out, in_, AF.Copy, reads, writes)
        return S.add(eng, lambda e: e.tensor_copy(out=out, in_=in_), reads, writes, dur=_vdur(eng, _fsize(out)))

    def memset(eng, ap, val, writes=()):
        return S.add(eng, lambda e: e.memset(ap, val), (), writes, dur=_vdur(eng, _fsize(ap), 0.5))

    def scan(out, d0, d1, reads=(), writes=()):
        return S.add("dve", lambda e: e.tensor_tensor_scan(out=out, data0=d0, data1=d1, initial=0.0,
                                                           op0=ALU.mult, op1=ALU.add), reads, writes,
                     dur=_vdur("dve", _fsize(out), 2.0))

    def reduce_x(out, in_, reads=(), writes=()):
        return S.add("dve", lambda e: e.tensor_reduce(out=out, in_=in_, axis=AX.X, op=ALU.add), reads, writes,
                     dur=_vdur("dve", _fsize(in_)))

    def recip(out, in_, reads=(), writes=()):
        return S.add("dve", lambda e: e.reciprocal(out=out, in_=in_), reads, writes, dur=_vdur("dve", _fsize(out)))

    def bk_bf(b):
        return banks[b][:].bitcast(BF16)

    def dump(name, ap_sb, shape, dt):
        d = nc.dram_tensor("dbg_" + name, list(shape), dt, kind="ExternalOutput").ap()
        dbg_d[name] = d
        dma("sp", d, ap_sb)

    identb = sbp("identb", [128, 128], BF16)
    trib = sbp("trib", [128, 128], BF16)
    negib = sbp("negib", [128, 128], BF16)
    gmaskf = sbp("gmaskf", [128, 256], F32)
    g1t = sbp("g1t", [128, 8], F32)
    g2t = sbp("g2t", [128, 8], F32)
    g3t = sbp("g3t", [128, 8], F32)
    neghalf = sbp("neghalf", [128, 16], F32)
    junk = sbp("junk", [128, 1024], BF16)
    st_ss = sbp("st_ss", [128, 16], F32)
    st_ms = sbp("st_ms", [128, 16], F32)
    st_rs = sbp("st_rs", [128, 16], F32)
    arena = sbp("arena", [128, 16384], F32)

    def aview(off, parts, shape_free, dt):
        n = 1
        for s_ in shape_free:
            n *= s_
        nbytes = n * (2 if dt == BF16 else 4)
        assert off % 4 == 0 and nbytes % 4 == 0 and off + nbytes <= 65536
        a = arena[0:parts, off // 4:(off + nbytes) // 4]
        if dt == BF16:
            a = a.bitcast(BF16)
        if len(shape_free) == 2:
            a = a.rearrange("p (a b) -> p a b", b=shape_free[1])
        elif len(shape_free) == 3:
            a = a.rearrange("p (a b c) -> p a b c", b=shape_free[1], c=shape_free[2])
        return a

    dma("pool", identb[:], dr["ident"], writes=["ident"])
    dma("pool", trib[:], dr["tri"], writes=["tri"])
    dma("pool", negib[:], dr["negi"], writes=["negi"])
    dma("sp", gmaskf[:], dr["gmask"], writes=["gmask"])
    dma("sp", g1t[:], dr["g1"], writes=["g1"])
    dma("sp", g2t[:], dr["g2"], writes=["g2"])
    dma("sp", g3t[:], dr["g3"], writes=["g3"])
    memset("pool", neghalf[:], -0.5, writes=["neghalf"])

    def norm_to_hT(src, gain, gkey, hT, hkey, tag, htm_bufs):
        htm = _Rot(htm_bufs)
        trb = _Rot(list(range(8)))
        for tb in range(NB):
            used = []
            for ti in range(4):
                t = tb * 4 + ti
                xa, xk = src(t)
                hb, hbk = htm.next()
                used.append((hb, hbk))
                act(junk[:], xa, AF.Square, reads=xk, writes=[(tag, "ss", t), "junk"], accum_out=st_ss[:, t:t + 1])
                ts("dve", st_ms[:, t:t + 1], st_ss[:, t:t + 1], 1.0 / D, EPS, ALU.mult, ALU.add,
                   reads=[(tag, "ss", t)], writes=[(tag, "ms", t)])
                tt("pool", st_rs[:, t:t + 1], st_ms[:, t:t + 1], neghalf[:, 0:1], ALU.pow,
                   reads=[(tag, "ms", t), "neghalf"], writes=[(tag, "rstd", t)])
                ts("dve", hb, xa, st_rs[:, t:t + 1], None, ALU.mult,
                   reads=list(xk) + [(tag, "rstd", t)], writes=[hbk])
            for c in range(KC):
                b = trb.next()
                for ti in range(4):
                    hb, hbk = used[ti]
                    tr(bk_bf(b)[:, ti * 128:(ti + 1) * 128], hb[:, c * 128:(c + 1) * 128],
                       reads=[hbk], writes=[("ps", b)])
                if c % 2 == 0:
                    act(hT[:, c, tb * 512:(tb + 1) * 512], bk_bf(b)[:, 0:512], AF.Copy,
                        reads=[("ps", b), gkey], writes=[(hkey, c, tb)], scale=gain[:, c:c + 1])
                else:
                    ts("dve", hT[:, c, tb * 512:(tb + 1) * 512], bk_bf(b)[:, 0:512], gain[:, c:c + 1], None, ALU.mult,
                       reads=[("ps", b), gkey], writes=[(hkey, c, tb)])

    es_h = ExitStack()
    hT = mk_alloc(es_h)("hT", [128, KC, S_LEN], BF16)
    es_mix = ExitStack()
    sbm = mk_alloc(es_mix)
    yTg = sbm("yTg", [128, 4, S_LEN], BF16)
    if lvl >= 2:
        wv = sbm("wv", [128, KC, 512], BF16)
        wqk = [sbm(f"wqk{i}", [128, KC, 256], BF16) for i in range(2)]
        wf = sbm("wf", [128, KC, 8], BF16)
        def fox_prefetch():
            dma("pool", wf[:], dr["w_f"], writes=["wf"])
            for c in range(KC):
                dma("pool", wv[:, c, :], dr["w_fv"][:, c, :], writes=[("wv", c)])
            for c in range(KC):
                dma("pool", wqk[0][:, c, :], dr["w_fqk"][0, :, c, :], writes=[("wqk", 0, c)])

    if lvl >= 1:
        es2 = ExitStack()
        sb2 = mk_alloc(es2)
        wg = sb2("wg", [128, KC, 1552], BF16)
        lrT = sb2("lrT", [16, S_LEN], BF16)
        lrw = sb2("lrw", [16, 256], BF16)
        lrb = sb2("lrb", [128, 2], F32)
        nlrb = sb2("nlrb", [128, 2], F32)
        gonbc = sb2("gonbc", [128, 512], F32)
        msk = sb2("msk", [128, S_LEN], BF16)
        dec = sb2("dec", [128, 2, 32], F32)
        qeT = sb2("qeT", [128, 2, S_LEN], BF16)
        keT = sb2("keT", [128, 2, S_LEN], BF16)
        etmp = [sb2(f"etmp{i}", [128, 512], F32) for i in range(6)]
        sst = sb2("sst", [128, 2, 256], F32)
        sbf = [sb2(f"sbf{i}", [128, 2, 256], BF16) for i in range(2)]
        atsb = [sb2(f"atsb{i}", [128, 256], BF16) for i in range(2)]
        sqt = [sb2(f"sqt{i}", [128, 512], F32) for i in range(2)]
        ssq = sb2("ssq", [128, NT, 4], F32)
        msq = sb2("msq", [128, NT, 4], F32)
        rsq = sb2("rsq", [128, NT, 4], F32)
        ytm = [sb2(f"ytm{i}", [128, 512], BF16) for i in range(2)]
        vg = aview(0, 128, [NT, 512], BF16)
        gsog = aview(16384, 128, [NT, 512], BF16)
        sp_ = aview(32768, 128, [2, S_LEN], F32)
        cs = aview(49152, 128, [2, S_LEN], F32)
        kdT = aview(32768, 128, [2, S_LEN], BF16)
        kdtm = aview(40960, 128, [NT, 256], BF16)
        erot = _Rot(list(range(6)))

        for c in range(KC):
            dma("pool", wg[:, c, :], dr["w_gla"][:, c, :], writes=[("wg", c)])
        dma("pool", lrw[:], dr["lr_w"], writes=["lrw"])
        dma("sp", lrb[:], dr["lr_b"], writes=["lrb"])
        dma("sp", gonbc[:], dr["gon_bc"], writes=["gonbc"])
        ts("dve", nlrb[:], lrb[:], -1.0, None, ALU.mult, reads=["lrb"], writes=["nlrb"])
        memset("pool", msk[:], 1.0, writes=["msk"])
        memset("pool", msk[:].rearrange("p (a b) -> p a b", b=64)[:, :, 0:1], 0.0, writes=["msk"])

    xts = [aview(i * 4096, 128, [D], F32) for i in range(3)]
    htm_bufs = [(aview(12288 + i * 2048, 128, [D], BF16), ("htm", i)) for i in range(8)]
    xrot = _Rot(list(range(3)))

    def src1(t):
        r = xrot.next()
        dma("sp", xts[r], dr["x"][t * 128:(t + 1) * 128, :], writes=[("xt", r)])
        return xts[r], [("xt", r)]

    norm_to_hT(src1, g1t, "g1", hT, "hT", "n1", htm_bufs)
    A1_KEYS = [("xt", r) for r in range(3)] + [("htm", i) for i in range(8)]
    if stop == "hT":
        S.barrier()
    if stop == "hT":
        dump("hT", hT[:], [128, KC, S_LEN], BF16)

    if lvl >= 1:
        prot = _Rot(list(range(8)))

        if GLA_LIMIT >= 1:
            for tb in range(NB):
                tbs = slice(tb * 512, (tb + 1) * 512)
                b = prot.next()
                for c in range(KC):
                    mm(banks[b][0:16, :], wg[:, c, 1536:1552], hT[:, c, tbs], c == 0, c == KC - 1,
                       reads=[("wg", c), ("hT", c, tb)], writes=[("ps", b)])
                cp("act", lrT[:, tbs], banks[b][0:16, :], reads=[("ps", b)], writes=[("lrT", tb)])
        if GLA_LIMIT >= 2:
            for f in range(2):
                for tb in range(NB):
                    tbs = slice(tb * 512, (tb + 1) * 512)
                    b = prot.next()
                    mm(banks[b][:, :], lrw[:, f * 128:(f + 1) * 128], lrT[:, tbs], True, True,
                       reads=["lrw", ("lrT", tb)], writes=[("ps", b)])
                    r = erot.next()
                    act(etmp[r][:], banks[b][:, :], AF.Exp, reads=[("ps", b), "nlrb"], writes=[("etmp", r)],
                        scale=-1.0, bias=nlrb[:, f:f + 1])
                    act(sp_[:, f, tbs], etmp[r][:], AF.Ln, reads=[("etmp", r)], writes=[("sp", f)], bias=1.0)
                scan(cs[:, f, :], msk[:], sp_[:, f, :], reads=[("sp", f), "msk"], writes=[("cs", f)])
                act(dec[:, f, :], cs[:, f, :].rearrange("p (n c) -> p n c", c=64)[:, :, 63],
                    AF.Exp, reads=[("cs", f)], writes=[("dec", f)], scale=-1.0 / 16)
            if lvl >= 2:
                fox_prefetch()
        if GLA_LIMIT >= 3:
            LNQ = math.log(0.125)
            for f in range(2):
                for tb in range(NB):
                    tbs = slice(tb * 512, (tb + 1) * 512)
                    r1, r2, r3 = erot.next(), erot.next(), erot.next()
                    csv = cs[:, f, tbs]
                    act(etmp[r1][:], csv, AF.Exp, reads=[("cs", f)], writes=[("etmp", r1)], scale=-1.0 / 16, bias=LNQ)
                    act(etmp[r2][:], csv, AF.Exp, reads=[("cs", f)], writes=[("etmp", r2)], scale=1.0 / 16)
                    cs3 = csv.rearrange("p (n c) -> p n c", c=64)
                    tt("dve", etmp[r3][:].rearrange("p (n c) -> p n c", c=64),
                       cs3[:, :, 63:64].to_broadcast([128, 8, 64]), cs3, ALU.subtract,
                       reads=[("cs", f)], writes=[("etmp", r3)])
                    act(etmp[r3][:], etmp[r3][:], AF.Exp, reads=[("etmp", r3)], writes=[("etmp", r3)], scale=-1.0 / 16)
                    b = prot.next()
                    for c in range(KC):
                        mm(banks[b][:, :], wg[:, c, f * 128:(f + 1) * 128], hT[:, c, tbs], c == 0, c == KC - 1,
                           reads=[("wg", c), ("hT", c, tb)], writes=[("ps", b)])
                    tt("dve", qeT[:, f, tbs], banks[b][:, :], etmp[r1][:], ALU.mult,
                       reads=[("ps", b), ("etmp", r1)], writes=[("qeT", f, tb)])
                    b = prot.next()
                    for c in range(KC):
                        mm(banks[b][:, :], wg[:, c, 256 + f * 128:256 + (f + 1) * 128], hT[:, c, tbs], c == 0, c == KC - 1,
                           reads=[("wg", c), ("hT", c, tb)], writes=[("ps", b)])
                    tt("dve", keT[:, f, tbs], banks[b][:, :], etmp[r2][:], ALU.mult,
                       reads=[("ps", b), ("etmp", r2)], writes=[("keT", f, tb)])
                    tt("dve", kdT[:, f, tbs], banks[b][:, :], etmp[r3][:], ALU.mult,
                       reads=[("ps", b), ("etmp", r3)], writes=[("kdT", f, tb), ("sp", 0), ("sp", 1)])
        if GLA_LIMIT >= 4:
            for t in range(NT):
                b = prot.next()
                for f in range(2):
                    tr(bk_bf(b)[:, f * 128:(f + 1) * 128], kdT[:, f, t * 128:(t + 1) * 128],
                       reads=[("kdT", f, t // 4)], writes=[("ps", b)])
                cp("act", kdtm[:, t, :], bk_bf(b)[:, 0:256], reads=[("ps", b)], writes=[("kdtm", t), ("sp", 0), ("sp", 1)])
        if GLA_LIMIT >= 5:
            for t in range(NT):
                tsl = slice(t * 128, (t + 1) * 128)
                b = prot.next()
                for c in range(KC):
                    mm(banks[b][:, :], hT[:, c, tsl], wg[:, c, 512:1024], c == 0, c == KC - 1,
                       reads=[("wg", c), ("hT", c, t // 4)], writes=[("ps", b)])
                cp("act", vg[:, t, :], banks[b][:, :], reads=[("ps", b)], writes=[("vg", t)] + A1_KEYS)
                b = prot.next()
                for c in range(KC):
                    mm(banks[b][:, :], hT[:, c, tsl], wg[:, c, 1024:1536], c == 0, c == KC - 1,
                       reads=[("wg", c), ("hT", c, t // 4)], writes=[("ps", b)])
                r = erot.next()
                act(etmp[r][:], banks[b][:, :], AF.Silu, reads=[("ps", b)], writes=[("etmp", r)])
                tt("pool", gsog[:, t, :], etmp[r][:], gonbc[:], ALU.mult,
                   reads=[("etmp", r), "gonbc"], writes=[("gsog", t)] + A1_KEYS)
        if GLA_LIMIT >= 6:
            A_BK, B_BK = 0, (1, 2)
            prot = _Rot([3, 4, 5, 6, 7])
            memset("pool", sbf[1][:], 0.0, writes=[("sbf", 1, 0), ("sbf", 1, 1)])
            for n in range(32):
                t, half = n // 2, n % 2
                r0 = half * 64
                rs = slice(r0, r0 + 64)
                ns = slice(n * 64, (n + 1) * 64)
                tb = n // 8
                dsb = []
                for hp in range(2):
                    b = prot.next()
                    dsb.append(b)
                    mm(banks[b][:, 0:256], kdtm[rs, t, hp * 128:(hp + 1) * 128], vg[rs, t, hp * 256:(hp + 1) * 256],
                       True, True, reads=[("kdtm", t), ("vg", t)], writes=[("ps", b)])
                ar = n % 2
                for i in range(2):
                    ab = prot.next()
                    ps_ = slice(i * 64, (i + 1) * 64)
                    for hp in range(2):
                        mm(banks[ab][rs, hp * 64:(hp + 1) * 64], keT[ps_, hp, ns], qeT[ps_, hp, ns], True, True,
                           reads=[("keT", hp, tb), ("qeT", hp, tb)], writes=[("ps", ab)])
                    tt("dve", atsb[ar][rs, i * 128:(i + 1) * 128], banks[ab][rs, 0:128], gmaskf[rs, 0:128], ALU.mult,
                       reads=[("ps", ab), "gmask"], writes=[("atsb", ar, i)])
                for h in range(4):
                    hp, i = h // 2, h % 2
                    ps_ = slice(i * 64, (i + 1) * 64)
                    mm(banks[A_BK][rs, h * 128:(h + 1) * 128], atsb[ar][rs, i * 128 + hp * 64:i * 128 + (hp + 1) * 64],
                       vg[rs, t, h * 128:(h + 1) * 128], True, True,
                       reads=[("atsb", ar, i), ("vg", t)], writes=[("ps", A_BK)])
                for h in range(4):
                    hp, i = h // 2, h % 2
                    ps_ = slice(i * 64, (i + 1) * 64)
                    mm(banks[B_BK[i]][rs, hp * 128:(hp + 1) * 128], qeT[ps_, hp, ns],
                       sbf[(n - 1) % 2][ps_, hp, i * 128:(i + 1) * 128], True, True,
                       reads=[("qeT", hp, tb), ("sbf", (n - 1) % 2, hp)], writes=[("ps", B_BK[i])])
                if n < 31:
                    for hp in range(2):
                        b = dsb[hp]
                        if n == 0:
                            cp("dve", sst[:, hp, :], banks[b][:, 0:256], reads=[("ps", b)], writes=[("sst", hp)])
                        else:
                            stt(sst[:, hp, :], sst[:, hp, :], dec[:, hp, n:n + 1], banks[b][:, 0:256], ALU.mult, ALU.add,
                                reads=[("ps", b), ("sst", hp), ("dec", hp)], writes=[("sst", hp)])
                        cp("act", sbf[n % 2][:, hp, :], sst[:, hp, :], reads=[("sst", hp)], writes=[("sbf", n % 2, hp)])
                if half == 1:
                    qr = t % 2
                    osb = etmp[qr]
                    for i in range(2):
                        cp("act", osb[:].rearrange("p (hp i c) -> p hp i c", hp=2, i=2)[:, :, i, :],
                           banks[B_BK[i]][:, 0:256].rearrange("p (hp c) -> p hp c", hp=2),
                           reads=[("ps", B_BK[i])], writes=[("etmp", qr)])
                    tt("dve", osb[:], osb[:], banks[A_BK][:, :], ALU.add,
                       reads=[("ps", A_BK), ("etmp", qr)], writes=[("etmp", qr)])
                    act(sqt[qr][:], osb[:], AF.Square, reads=[("etmp", qr)], writes=[("sqt", qr)])
                    reduce_x(ssq[:, t, :], sqt[qr][:].rearrange("p (a b) -> p a b", b=128),
                             reads=[("sqt", qr)], writes=[("ssq", t)])
                    ts("dve", msq[:, t, :], ssq[:, t, :], 1.0 / 128, EPS, ALU.mult, ALU.add,
                       reads=[("ssq", t)], writes=[("msq", t)])
                    tt("pool", rsq[:, t, :], msq[:, t, :], neghalf[:, 0:4], ALU.pow,
                       reads=[("msq", t), "neghalf"], writes=[("rsq", t)])
                    tt("pool", osb[:].rearrange("p (a b) -> p a b", b=128), osb[:].rearrange("p (a b) -> p a b", b=128),
                       rsq[:, t, :].unsqueeze(2).to_broadcast([128, 4, 128]), ALU.mult,
                       reads=[("etmp", qr), ("rsq", t)], writes=[("etmp", qr)])
                    tt("pool", ytm[qr][:], osb[:], gsog[:, t, :], ALU.mult,
                       reads=[("etmp", qr), ("gsog", t)], writes=[("ytm", qr)])
                    b = prot.next()
                    for h in range(4):
                        tr(bk_bf(b)[:, h * 128:(h + 1) * 128], ytm[qr][:, h * 128:(h + 1) * 128],
                           reads=[("ytm", qr)], writes=[("ps", b)])
                    cp("act", yTg[:, :, t * 128:(t + 1) * 128],
                       bk_bf(b)[:, 0:512].rearrange("p (a b) -> p a b", b=128),
                       reads=[("ps", b)], writes=[("yTg", t)])
        S.barrier()
        es2.close()
        if stop == "gla":
            dump("yTg", yTg[:], [128, 4, S_LEN], BF16)

    if lvl >= 2:
        es_f = ExitStack()
        yTf = mk_alloc(es_f)("yTf", [128, 4, S_LEN], BF16)
        cp3t = mk_alloc(es_f)("cp3t", [128, 3, S_LEN], BF16)
        es3 = ExitStack()
        sb3 = mk_alloc(es3)
        QTs = [aview(0, 128, [2, S_LEN], BF16), aview(8192, 128, [2, S_LEN], BF16)]
        KTs = [aview(16384, 128, [2, S_LEN], BF16), aview(24576, 128, [2, S_LEN], BF16)]
        vf = aview(32768, 128, [NT, 8, 128], BF16)
        spf = sb3("spf", [8, S_LEN], F32)[:]
        cp3 = cp3t[0:8, :, :]
        cpart = [cp3[:, i, :] for i in range(3)]
        gqt = sb3("gqt", [64, 1], F32)
        gq8 = sb3("gq8", [64, 1], F32)
        gkt = sb3("gkt", [64, 1], F32)
        ss8 = sb3("ss8", [128, 2, NT, 4], F32)
        ms8 = sb3("ms8", [128, 2, NT, 4], F32)
        rs8 = sb3("rs8", [128, 2, NT, 4], F32)
        dma("sp", gqt[:], dr["gq"], writes=["gqt"])
        dma("sp", gkt[:], dr["gk"], writes=["gkt"])
        QTf = aview(0, 128, [4 * S_LEN], BF16)
        KTf = aview(16384, 128, [4 * S_LEN], BF16)
        ts("dve", gq8[:], gqt[:], 0.125, None, ALU.mult, reads=["gqt"], writes=["gq8"])
        prot = _Rot(list(range(8)))
        vfk = [("vf", t) for t in range(NT)]
        memset("pool", aview(32768, 128, [NT * 8 * 128], BF16), 0.0, writes=vfk)
        vf4 = aview(32768, 128, [NT * 4, 2, 128], BF16)
        memset("pool", vf4[:, :, 0, 64:65], 1.0, writes=vfk)
        memset("pool", vf4[:, :, 1, 63:64], 1.0, writes=vfk)

        bft = sb3("bft", [8, 1], F32)
        nbf = sb3("nbf", [8, 1], F32)
        tmp8 = [sb3(f"tmp8{i}", [8, 512], F32) for i in range(1)]
        cneg = sb3("cneg", [8, S_LEN], F32)[:]
        dma("sp", bft[:], dr["b_f"], writes=["bft"])
        ts("dve", nbf[:], bft[:], -1.0, None, ALU.mult, reads=["bft"], writes=["nbf"])
        for tb in range(NB):
            tbs = slice(tb * 512, (tb + 1) * 512)
            b = prot.next()
            for c in range(KC):
                mm(banks[b][0:8, :], wf[:, c, :], hT[:, c, tbs], c == 0, c == KC - 1,
                   reads=["wf"], writes=[("ps", b)])
            r = 0
            act(tmp8[r][:], banks[b][0:8, :], AF.Exp, reads=[("ps", b), "nbf"], writes=[("tmp8", r)],
                scale=-1.0, bias=nbf[:, 0:1])
            act(spf[:, tbs], tmp8[r][:], AF.Ln, reads=[("tmp8", r)], writes=["spf"], bias=1.0)
        S.add("dve", lambda e: e.tensor_tensor_scan(out=cneg, data0=spf, data1=spf, initial=0.0,
                                                    op0=ALU.add, op1=ALU.max),
              reads=["spf"], writes=["cneg"], dur=4400.0)
        cr = cneg
        cp("dve", cpart[0], cneg, reads=["cneg"], writes=[("cpart", 0)])
        tt("dve", cr, cneg, cpart[0], ALU.subtract, reads=["cneg", ("cpart", 0)], writes=["cneg"])
        cp("dve", cpart[1], cr, reads=["cneg"], writes=[("cpart", 1)])
        tt("dve", cr, cr, cpart[1], ALU.subtract, reads=["cneg", ("cpart", 1)], writes=["cneg"])
        cp("dve", cpart[2], cr, reads=["cneg"], writes=[("cpart", 2)])
        memset("dve", QTf[64:96, :], 0.0, writes=["QTpad"])
        memset("pool", QTf[96:128, :], 1.0, writes=["QTpad"])
        memset("dve", KTf[64:96, :], -1.0, writes=["KTpad"])
        memset("pool", KTf[96:128, :], 0.0, writes=["KTpad"])

        sq2 = [sb3(f"sq2{i}", [128, 256], F32) for i in range(3)]
        qkn = [sb3(f"qkn{i}", [128, 256], BF16) for i in range(3)]
        qkraw = [sb3(f"qkraw{i}", [128, 256], F32) for i in range(3)]
        pt = [sb3(f"pt{i}", [128, 512], BF16) for i in range(4)]
        NY = 4
        yun = [sb3(f"yun{i}", [128, 512], F32) for i in range(NY)]
        rs4 = [sb3(f"rs4{i}", [128, 4], F32) for i in range(NY)]
        rc4 = [sb3(f"rc4{i}", [128, 4], F32) for i in range(NY)]
        rrt = sb3("rrt", [NY, 512], F32)
        rr = [rrt[i:i + 1, :] for i in range(NY)]
        rbc = [sb3(f"rbc{i}", [128, 512], F32) for i in range(3)]

        obanks = _Rot([0, 1])
        sbanks = _Rot([2, 3, 4, 5, 6, 7])
        ptrot = _Rot(list(range(4)))

        for t in range(NT):
            tsl = slice(t * 128, (t + 1) * 128)
            bv = sbanks.next()
            for c in range(KC):
                mm(banks[bv][:, :], hT[:, c, tsl], wv[:, c, :], c == 0, c == KC - 1,
                   reads=[("wv", c)], writes=[("ps", bv)])
            for par in range(2):
                cp("dve", vf[:, t, :, :].rearrange("p (pr two) c -> p pr two c", two=2)[:, :, par, par * 64:(par + 1) * 64],
                   banks[bv][:, :].rearrange("p (pr two d) -> p pr two d", two=2, d=64)[:, :, par, :],
                   reads=[("ps", bv)], writes=[("vf", t)])

        qk_count = [0]

        def proj_tile(pr, t):
            sl = pr % 2
            tsl = slice(t * 128, (t + 1) * 128)
            bqk = sbanks.next()
            for c in range(KC):
                mm(banks[bqk][:, 0:256], hT[:, c, tsl], wqk[sl][:, c, :], c == 0, c == KC - 1,
                   reads=[("wqk", sl, c)], writes=[("ps", bqk)])
            qr = qk_count[0] % 3
            qk_count[0] += 1
            act(sq2[qr][:], banks[bqk][:, 0:256], AF.Square, reads=[("ps", bqk)], writes=[("sq2", qr)])
            cp("act", qkraw[qr][:], banks[bqk][:, 0:256], reads=[("ps", bqk)], writes=[("qkraw", qr)])
            reduce_x(ss8[:, sl, t, :], sq2[qr][:].rearrange("p (a b) -> p a b", b=64),
                     reads=[("sq2", qr)], writes=[("ss8", sl, t)])
            ts("dve", ms8[:, sl, t, :], ss8[:, sl, t, :], 1.0 / 64, EPS, ALU.mult, ALU.add,
               reads=[("ss8", sl, t)], writes=[("ms8", sl, t)])
            tt("pool", rs8[:, sl, t, :], ms8[:, sl, t, :], neghalf[:, 0:4], ALU.pow,
               reads=[("ms8", sl, t), "neghalf"], writes=[("rs8", sl, t)])
            tt("dve", qkn[qr][:].rearrange("p (a b) -> p a b", b=64),
               qkraw[qr][:].rearrange("p (a b) -> p a b", b=64),
               rs8[:, sl, t, :].unsqueeze(2).to_broadcast([128, 4, 64]), ALU.mult,
               reads=[("qkraw", qr), ("rs8", sl, t)], writes=[("qkn", qr)])
            bt = sbanks.next()
            for i in range(4):
                tr(bk_bf(bt)[0:64, i * 128:(i + 1) * 128], qkn[qr][:, i * 64:(i + 1) * 64],
                   reads=[("qkn", qr)], writes=[("ps", bt)])
            act(QTs[sl][0:64, :, tsl], bk_bf(bt)[0:64, 0:256].rearrange("p (a b) -> p a b", b=128), AF.Copy,
                reads=[("ps", bt), "gq8"], writes=[("QT", sl, t)], scale=gq8[:, 0:1])
            act(KTs[sl][0:64, :, tsl], bk_bf(bt)[0:64, 256:512].rearrange("p (a b) -> p a b", b=128), AF.Copy,
                reads=[("ps", bt), "gkt"], writes=[("KT", sl, t)], scale=gkt[:, 0:1])

        def pair_setup(pr):
            sl = pr % 2
            if pr >= 1:
                for c in range(KC):
                    dma("pool", wqk[sl][:, c, :], dr["w_fqk"][pr, :, c, :], writes=[("wqk", sl, c)])
            for hh2 in range(2):
                h = 2 * pr + hh2
                dma("sp", QTs[sl][64:67, hh2, :], cp3[h:h + 1, :, :],
                    reads=[("cpart", 0), ("cpart", 1), ("cpart", 2), "QTpad"], writes=[("QTa", sl, hh2)])
                dma("sp", KTs[sl][96:99, hh2, :], cp3[h:h + 1, :, :],
                    reads=[("cpart", 0), ("cpart", 1), ("cpart", 2), "KTpad"], writes=[("KTa", sl, hh2)])

        unit_count = [0]

        def attn_unit(pr, hh2, qb):
            sl = pr % 2
            h = 2 * pr + hh2
            QT, KT = QTs[sl], KTs[sl]
            bo = obanks.next()
            nk = 4 * qb + 4
            qdeps = [("QT", sl, t) for t in range(qb * 4, qb * 4 + 4)] + [("QTa", sl, hh2), "QTpad", "KTpad"]
            for kt in range(nk):
                j = kt - 4 * qb
                c0 = 128 * j if j > 0 else 0
                N = 512 - c0
                bs = sbanks.next()
                mm(banks[bs][:, 0:N], KT[:, hh2, kt * 128:(kt + 1) * 128],
                   QT[:, hh2, qb * 512 + c0:(qb + 1) * 512], True, j < 0,
                   reads=[("KT", sl, kt), ("KTa", sl, hh2)] + qdeps, writes=[("ps", bs)])
                if j >= 0:
                    mm(banks[bs][:, 0:128], negib[:], trib[:], False, True,
                       reads=["negi", "tri"], writes=[("ps", bs)])
                pr_ = ptrot.next()
                act(pt[pr_][:, 0:N], banks[bs][:, 0:N], AF.Exp, reads=[("ps", bs)], writes=[("pt", pr_)])
                mm(banks[bo][:, c0:512], vf[:, kt, h, :], pt[pr_][:, 0:N], kt == 0, kt == nk - 1,
                   reads=[("vf", kt), ("pt", pr_)], writes=[("ps", bo)])
            u = unit_count[0]
            unit_count[0] += 1
            yr = u % NY
            par = h % 2
            ev = slice(0, 65) if par == 0 else slice(0, 128)
            rrow = 64 if par == 0 else 63
            ys = slice(par * 64, (par + 1) * 64)
            cp("dve", yun[yr][ev, :], banks[bo][ev, :], reads=[("ps", bo)], writes=[("yun", yr)])
            dma("sp", rs4[yr][:], yun[yr][rrow:rrow + 1, :], reads=[("yun", yr)], writes=[("rs4", yr)])
            recip(rc4[yr][:], rs4[yr][:], reads=[("rs4", yr)], writes=[("rc4", yr)])
            dma("sp", rr[yr], rc4[yr][:], reads=[("rc4", yr)], writes=[("rr", yr)])
            br = u % 3
            dma("sp", rbc[br][ys, :], rr[yr].unsqueeze(1).to_broadcast([1, 64, 512]), reads=[("rr", yr)], writes=[("rbc", br)])
            tt("dve", yTf[ys, h // 2, qb * 512:(qb + 1) * 512], yun[yr][ys, :], rbc[br][ys, :], ALU.mult,
               reads=[("yun", yr), ("rbc", br)], writes=[("yTf", h, qb)])

        pair_setup(0)
        for t in range(NT):
            proj_tile(0, t)
        for pr in range(4):
            if pr + 1 < 4:
                pair_setup(pr + 1)
            if pr == 3 and lvl >= 3:
                wog = wv[:].rearrange("p a b -> p (a b)")[:, 0:4 * D].rearrange("p (h d) -> p h d", d=D)
                wof = cp3t[:].rearrange("p a b -> p (a b)")[:, 0:4 * D].rearrange("p (h d) -> p h d", d=D)
                for h in range(4):
                    dma("pool", wog[:, h, :], dr["wo_g"][:, h, :], writes=[("wog", h)] + [("wv", c) for c in range(KC)])
                for h in range(4):
                    dma("pool", wof[:, h, :], dr["wo_f"][:, h, :], writes=[("wof", h)] + [("cpart", j_) for j_ in range(3)])
            units = [(hh2, qb) for hh2 in range(2) for qb in range(NB)]
            for ui, (hh2, qb) in enumerate(units):
                attn_unit(pr, hh2, qb)
                if pr + 1 < 4:
                    proj_tile(pr + 1, 2 * ui)
                    proj_tile(pr + 1, 2 * ui + 1)
        S.barrier()
        es3.close()
        if stop == "fox":
            dump("yTf", yTf[:], [128, 4, S_LEN], BF16)

    x1 = arena[:].rearrange("p (t d) -> p t d", d=D)

    if lvl >= 3:
        es_b = ExitStack()
        sbb = mk_alloc(es_b)
        for t in range(NT):
            dma("sp", x1[:, t, :], dr["x"][t * 128:(t + 1) * 128, :], writes=[("x1", t)])
        htm2 = [(sbb(f"htmc{i}", [128, D], BF16)[:], ("htm", i)) for i in range(8)]
        prot = _Rot(list(range(8)))
        for t in range(NT):
            tsl = slice(t * 128, (t + 1) * 128)
            for nh in range(2):
                nsl = slice(nh * 512, (nh + 1) * 512)
                b = prot.next()
                for h in range(4):
                    mm(banks[b][:, :], yTg[:, h, tsl], wog[:, h, nsl], h == 0, False,
                       reads=[("wog", h)], writes=[("ps", b)])
                for h in range(4):
                    mm(banks[b][:, :], yTf[:, h, tsl], wof[:, h, nsl], False, h == 3,
                       reads=[("wof", h)], writes=[("ps", b)])
                tt("dve", x1[:, t, nsl], x1[:, t, nsl], banks[b][:, :], ALU.add,
                   reads=[("ps", b), ("x1", t)], writes=[("x1", t)])
        if lvl >= 4:
            norm_to_hT(lambda t: (x1[:, t, :], [("x1", t)]), g2t, "g2", hT, "hT", "n2", htm2)
        S.barrier()
        es_b.close()
    if lvl >= 2:
        es_f.close()
    es_mix.close()
    if stop == "wo":
        dump("x1", arena[:], [128, 16384], F32)

    if lvl >= 4:
        es_c = ExitStack()
        sbc = mk_alloc(es_c)
        h2T = hT
        htm3 = [(sbc(f"htmd{i}", [128, D], BF16)[:], ("htm", i)) for i in range(8)]
        gbuf = sbc("gbuf", [128, GMAX, S_LEN], BF16)
        wu = [sbc(f"wu{i}", [128, KC, 256], BF16) for i in range(3)]
        wd = sbc("wd", [128, GMAX, D], BF16)
        cwg = sbc("s_cwg", [128, NJ, 4], F32)
        cwv = sbc("s_cwv", [128, NJ, 4], F32)
        ug = [sbc(f"ug{i}", [128, 1026], F32) for i in range(2)]
        uv = [sbc(f"uv{i}", [128, 1026], F32) for i in range(2)]
        cg = [sbc(f"cg{i}", [128, 512], F32) for i in range(2)]
        cv = [sbc(f"cv{i}", [128, 512], F32) for i in range(2)]
        sg = [sbc(f"sg{i}", [128, 512], F32) for i in range(2)]
        ptmp = sg
        dma("sp", cwg[:], dr["cwg"], writes=["cwg"])
        dma("sp", cwv[:], dr["cwv"], writes=["cwv"])
        for j in range(min(2, NJ)):
            dma("pool", wu[j % 3][:], dr["w_up"][j], writes=[("wu", j % 3)])
        for j in range(FFN_GROUPS[0][0], FFN_GROUPS[0][1]):
            dma("pool", wd[:, j, :], dr["w_down"][j], writes=[("wd", j)])
        prot = _Rot(list(range(8)))
        urot = 0
        crot = 0
        WU_AHEAD = 2
        for (j0, j1) in FFN_GROUPS:
            for j in range(j0, j1):
                if j0 > 0:
                    dma("pool", wd[:, j - j0, :], dr["w_down"][j], writes=[("wd", j - j0)])
            for j in range(j0, j1):
                jj = j - j0
                if j + WU_AHEAD < NJ:
                    dma("pool", wu[(j + WU_AHEAD) % 3][:], dr["w_up"][j + WU_AHEAD], writes=[("wu", (j + WU_AHEAD) % 3)])
                w = wu[j % 3]
                for half in range(2):
                    ur = urot % 2
                    urot += 1
                    bufs = ((ug[ur], ("ug", ur), ug[1 - ur], ("ug", 1 - ur), 0, cwg, "cwg"),
                            (uv[ur], ("uv", ur), uv[1 - ur], ("uv", 1 - ur), 128, cwv, "cwv"))
                    for (ub, ubk, uprev, uprevk, wo_, cw, cwk) in bufs:
                        if half == 0:
                            memset("pool", ub[:, 0:2], 0.0, writes=[ubk])
                        else:
                            cp("pool", ub[:, 0:2], uprev[:, 1024:1026], reads=[uprevk], writes=[ubk])
                        for sub in range(2):
                            tb = half * 2 + sub
                            tbs = slice(tb * 512, (tb + 1) * 512)
                            b = prot.next()
                            for c in range(KC):
                                mm(banks[b][:, :], w[:, c, wo_:wo_ + 128], h2T[:, c, tbs], c == 0, c == KC - 1,
                                   reads=[("wu", j % 3), ("hT", c, tb)], writes=[("ps", b)])
                            cp("act", ub[:, 2 + sub * 512:2 + (sub + 1) * 512], banks[b][:, :],
                               reads=[("ps", b)], writes=[ubk])
                    for sub in range(2):
                        tb = half * 2 + sub
                        tbs = slice(tb * 512, (tb + 1) * 512)
                        cr_ = crot % 2
                        crot += 1
                        outs = []
                        for (ub, ubk, uprev, uprevk, wo_, cw, cwk), cb, cbk in ((bufs[0], cg[cr_], ("cg", cr_)), (bufs[1], cv[cr_], ("cv", cr_))):
                            o = sub * 512
                            if wo_ == 0:
                                act(cb[:], ub[:, o + 2:o + 514], AF.Identity, reads=[ubk, cwk], writes=[cbk],
                                    scale=cw[:, j, 2:3], bias=cw[:, j, 3:4])
                            else:
                                ts("pool", cb[:], ub[:, o + 2:o + 514], cw[:, j, 2:3], cw[:, j, 3:4], ALU.mult, ALU.add,
                                   reads=[ubk, cwk], writes=[cbk])
                            stt(cb[:], ub[:, o + 1:o + 513], cw[:, j, 1:2], cb[:], ALU.mult, ALU.add,
                                reads=[ubk, cwk, cbk], writes=[cbk])
                            stt(cb[:], ub[:, o:o + 512], cw[:, j, 0:1], cb[:], ALU.mult, ALU.add,
                                reads=[ubk, cwk, cbk], writes=[cbk])
                        act(sg[cr_][:], cg[cr_][:], AF.Silu, reads=[("cg", cr_)], writes=[("sg", cr_)])
                        tt("pool", gbuf[:, jj, tbs], sg[cr_][:], cv[cr_][:], ALU.mult,
                           reads=[("sg", cr_), ("cv", cr_)], writes=[("gbuf", jj, tb)])
            ng = j1 - j0
            for t in range(NT):
                tsl = slice(t * 128, (t + 1) * 128)
                for nh in range(2):
                    nsl = slice(nh * 512, (nh + 1) * 512)
                    b = prot.next()
                    for jj in range(ng):
                        mm(banks[b][:, :], gbuf[:, jj, tsl], wd[:, jj, nsl], jj == 0, jj == ng - 1,
                           reads=[("gbuf", jj, t // 4), ("wd", jj)], writes=[("ps", b)])
                    tt("dve", x1[:, t, nsl], x1[:, t, nsl], banks[b][:, :], ALU.add,
                       reads=[("ps", b), ("x1", t)], writes=[("x1", t)])
        if lvl >= 5:
            norm_to_hT(lambda t: (x1[:, t, :], [("x1", t)]), g3t, "g3", hT, "hT", "n3", htm3)
        S.barrier()
        es_c.close()
        if stop == "ffn":
            dump("x2", arena[:], [128, 16384], F32)

    if lvl >= 5:
        es_d = ExitStack()
        sbd = mk_alloc(es_d)
        h3T = hT
        wpg = sbd("wpg", [128, KC, D], BF16)
        wpe = sbd("wpe", [128, 2, D], BF16)
        pb = sbd("pb", [128, NT, 256], BF16)
        pT = sbd("pT", [128, 2, S_LEN], BF16)
        bpg = sbd("bpg", [128, D], F32)
        peg = sbd("peg", [128, D], F32)
        ess = sbd("ess", [128, NT, 2], F32)
        ems = sbd("ems", [128, NT], F32)
        ers = sbd("ers", [128, NT], F32)
        gpre = [sbd(f"gpre{i}", [128, 512], F32) for i in range(2)]
        sig = [sbd(f"sig{i}", [128, 512], F32) for i in range(2)]
        en = [sbd(f"en{i}", [128, 512], F32) for i in range(2)]
        ot = [sbd(f"ot{i}", [128, D], F32) for i in range(2)]
        for c in range(KC):
            dma("pool", wpg[:, c, :], dr["w_pg"][:, c, :], writes=[("wpg", c)])
        for c in range(2):
            dma("pool", wpe[:, c, :], dr["w_pe"][:, c, :], writes=[("wpe", c)])
        for t in range(NT):
            dma("pool", pb[:, t, :], dr["p"][t * 128:(t + 1) * 128, :], writes=[("pb", t)])
        dma("sp", bpg[:], dr["bpg_bc"], writes=["bpg"])
        dma("sp", peg[:], dr["peg_bc"], writes=["peg"])
        prot = _Rot(list(range(8)))
        for tb in range(NB):
            for kc in range(2):
                b = prot.next()
                for ti in range(4):
                    t = tb * 4 + ti
                    tr(bk_bf(b)[:, ti * 128:(ti + 1) * 128], pb[:, t, kc * 128:(kc + 1) * 128],
                       reads=[("pb", t)], writes=[("ps", b)])
                cp("act", pT[:, kc, tb * 512:(tb + 1) * 512], bk_bf(b)[:, 0:512], reads=[("ps", b)], writes=[("pT", kc, tb)])
        perot = _Rot([0, 1, 2, 3])
        grot = _Rot([4, 5, 6, 7])
        out_ops = []
        k = 0
        for t in range(NT):
            tsl = slice(t * 128, (t + 1) * 128)
            peb = []
            for nh in range(2):
                nsl = slice(nh * 512, (nh + 1) * 512)
                b = perot.next()
                peb.append(b)
                for kc in range(2):
                    mm(banks[b][:, :], pT[:, kc, tsl], wpe[:, kc, nsl], kc == 0, kc == 1,
                       reads=[("pT", kc, t // 4), ("wpe", kc)], writes=[("ps", b)])
                act(junk[:, 0:512], banks[b][:, :], AF.Square, reads=[("ps", b)], writes=[("ess", t, nh), "junk"],
                    accum_out=ess[:, t, nh:nh + 1])
            tt("dve", ems[:, t:t + 1], ess[:, t, 0:1], ess[:, t, 1:2], ALU.add,
               reads=[("ess", t, 0), ("ess", t, 1)], writes=[("ems", t)])
            ts("dve", ems[:, t:t + 1], ems[:, t:t + 1], 1.0 / D, EPS, ALU.mult, ALU.add,
               reads=[("ems", t)], writes=[("ems", t)])
            tt("pool", ers[:, t:t + 1], ems[:, t:t + 1], neghalf[:, 0:1], ALU.pow,
               reads=[("ems", t), "neghalf"], writes=[("ers", t)])
            orr = t % 2
            for nh in range(2):
                nsl = slice(nh * 512, (nh + 1) * 512)
                b = grot.next()
                for c in range(KC):
                    mm(banks[b][:, :], h3T[:, c, tsl], wpg[:, c, nsl], c == 0, c == KC - 1,
                       reads=[("wpg", c)], writes=[("ps", b)])
                r = k % 2
                k += 1
                tt("dve", gpre[r][:], banks[b][:, :], bpg[:, nsl], ALU.add,
                   reads=[("ps", b), "bpg"], writes=[("gpre", r)])
                act(sig[r][:], gpre[r][:], AF.Sigmoid, reads=[("gpre", r)], writes=[("sig", r)])
                stt(en[r][:], banks[peb[nh]][:, :], ers[:, t:t + 1], peg[:, nsl], ALU.mult, ALU.mult,
                    reads=[("ps", peb[nh]), ("ers", t), "peg"], writes=[("en", r)])
                tt("pool", en[r][:], en[r][:], sig[r][:], ALU.mult,
                   reads=[("en", r), ("sig", r)], writes=[("en", r)])
                tt("dve", ot[orr][:, nsl], en[r][:], x1[:, t, nsl], ALU.add,
                   reads=[("en", r), ("x1", t)], writes=[("ot", orr)])
            out_ops.append(dma("sp", out_d[t * 128:(t + 1) * 128, :], ot[orr][:], reads=[("ot", orr)]))
        S.barrier()
        es_d.close()

    S.barrier()
    S.finalize()
    with nc.Block() as block:
        @block.sync
        def _(e):
            S.emit_engine("sp", e, sems, dsems)
            last = {}
            for d in S.ops["sp"]:
                if d.dma:
                    last[d.dsem] = d
            for d in last.values():
                e.wait_ge(dsems["sp"][d.dsem], 16 * d.dgen)

        @block.scalar
        def _(e):
            S.emit_engine("act", e, sems, dsems)

        @block.vector
        def _(e):
            S.emit_engine("dve", e, sems, dsems)

        @block.gpsimd
        def _(e):
            S.emit_engine("pool", e, sems, dsems)

        @block.tensor
        def _(e):
            S.emit_engine("pe", e, sems, dsems)
    es_h.close()
    es_all.close()
    return nc, dbg_d


def _consts():
    ident = np.eye(128, dtype=np.float32)
    k = np.arange(128)
    tri = (k[:, None] > k[None, :]).astype(np.float32)
    negi = (-30000.0 * np.eye(128)).astype(np.float32)
    s_ = (np.arange(128) % 64)[:, None]
    c_ = (np.arange(256) % 64)[None, :]
    gmask = (s_ <= c_).astype(np.float32)
    return dict(ident=ident, tri=tri, negi=negi, gmask=gmask)


def _layout_shared(inp):
    f = lambda a: np.ascontiguousarray(np.asarray(a, dtype=np.float32))
    w_in = f(inp["w_in"])[0]
    w_in_l = w_in.reshape(8, 128, 3096).transpose(1, 0, 2)
    sh = {}
    sh["g1"] = f(f(inp["norm1_g"])[0].reshape(8, 128).T)
    sh["g2"] = f(f(inp["norm2_g"])[0].reshape(8, 128).T)
    sh["g3"] = f(f(inp["norm3_g"])[0].reshape(8, 128).T)
    sh["w_gla"] = f(w_in_l[:, :, 0:1552])
    wqk_ = []
    for pr in range(4):
        q = w_in_l[:, :, 1552 + pr * 128:1552 + (pr + 1) * 128]
        k = w_in_l[:, :, 2064 + pr * 128:2064 + (pr + 1) * 128]
        wqk_.append(np.concatenate([q, k], axis=2))
    sh["w_fqk"] = f(np.stack(wqk_, 0))
    sh["w_fv"] = f(w_in_l[:, :, 2576:3088])
    sh["w_f"] = f(w_in_l[:, :, 3088:3096])
    sh["lr_w"] = f(f(inp["gla_lr_w"])[0])
    sh["lr_b"] = f(f(inp["gla_lr_b"])[0].reshape(2, 128).T)
    sh["gon_bc"] = f(np.broadcast_to(np.tile(f(inp["gla_onorm_g"])[0], 4)[None, :], (128, 512)))
    sh["b_f"] = f(f(inp["fox_b_f"])[0].reshape(8, 1))
    sh["gq"] = f(f(inp["fox_qnorm_g"])[0].reshape(64, 1))
    sh["gk"] = f(f(inp["fox_knorm_g"])[0].reshape(64, 1))
    w_o = f(inp["w_o"])[0]
    sh["wo_g"] = f(w_o[:512].reshape(4, 128, 1024).transpose(1, 0, 2))
    sh["wo_f"] = f(w_o[512:].reshape(4, 128, 1024).transpose(1, 0, 2))
    w_up = f(inp["w_up"])[0].reshape(8, 128, 2 * DFF)
    blocks = []
    for j in range(NJ):
        g_ = w_up[:, :, j * 128:(j + 1) * 128]
        v_ = w_up[:, :, DFF + j * 128:DFF + (j + 1) * 128]
        blocks.append(np.concatenate([g_, v_], axis=2).transpose(1, 0, 2))
    sh["w_up"] = f(np.stack(blocks, 0))
    conv_w = f(inp["conv_w"])[0]
    conv_b = f(inp["conv_b"])[0]
    cw = np.concatenate([conv_w, conv_b[None, :]], axis=0)
    sh["cwg"] = f(cw[:, :DFF].reshape(4, NJ, 128).transpose(2, 1, 0))
    sh["cwv"] = f(cw[:, DFF:].reshape(4, NJ, 128).transpose(2, 1, 0))
    sh["w_down"] = f(f(inp["w_down"])[0].reshape(NJ, 128, 1024))
    sh["w_pe"] = f(f(inp["w_pe"])[0].reshape(2, 128, 1024).transpose(1, 0, 2))
    sh["w_pg"] = f(f(inp["w_pg"])[0].reshape(8, 128, 1024).transpose(1, 0, 2))
    sh["bpg_bc"] = f(np.broadcast_to(f(inp["b_pg"])[0][None, :], (128, 1024)))
    sh["peg_bc"] = f(np.broadcast_to(f(inp["pe_norm_g"])[0][None, :], (128, 1024)))
    sh.update(_consts())
    return sh


_CACHE = {}


def _get_program(stop="full"):
    if stop not in _CACHE:
        _CACHE[stop] = build_program(stop)
    return _CACHE[stop]


def run(inputs, stop="full", cores=NCORES):
    nc, dbg = _get_program(stop)
    sh = _layout_shared(inputs)
    x = np.asarray(inputs["x"], dtype=np.float32)
    p = np.asarray(inputs["p"], dtype=np.float32)[0]
    in_maps = []
    for b in range(cores):
        m = dict(sh)
        m["x"] = np.ascontiguousarray(x[b])
        m["p"] = np.ascontiguousarray(p[b])
        in_maps.append(m)
    res = run_bass_kernel_spmd(nc, in_maps, core_ids=list(range(cores)))
    return res


def kernel(**inputs):
    res = run(inputs, "full", NCORES)
    out = np.stack([np.asarray(r["out"], dtype=np.float32) for r in res.results], axis=0)
    return out
```
